# Optimizing a Trainium2 kernel written in Bass

```python
import math
import jax
import jax.numpy as jnp
from jax import lax
import numpy as np

D_MODEL = 1024
BATCH = 2
SEQ = 8192
DEPTH = 1

GRID_W = 64
CTX_LEN = 256
N_HEADS = 8
HEAD_DIM = 64
V_DIM = 2 * HEAD_DIM
QK_W = N_HEADS * 2 * HEAD_DIM
ATT_W = N_HEADS * V_DIM
CONV_W = D_MODEL
CONV_K = 3
D_FF = 2816
N_MOD = 9
Q_BLOCK = 128
ROPE_BASE = 10000.0
AXIS_DIM = HEAD_DIM // 2
N_FREQ = AXIS_DIM // 2
EPS = 1e-6
Q_OFF = 0
K_OFF = Q_OFF + QK_W
V_OFF = K_OFF + QK_W
B_OFF = V_OFF + ATT_W
C_OFF = B_OFF + CONV_W
X_OFF = C_OFF + CONV_W
GA_OFF = X_OFF + CONV_W
GB_OFF = GA_OFF + D_MODEL
IN_W = GB_OFF + D_MODEL

kernel_name = "hybrid_diffattn_shortconv_macaron_dit_layer"


def rms_norm(x, g):
    xf = x.astype(jnp.float32)
    y = xf * lax.rsqrt(jnp.mean(xf * xf, axis=-1, keepdims=True) + EPS)
    return (y * g.astype(jnp.float32)).astype(x.dtype)


def modulate(x, shift, scale):
    return x * (1.0 + scale) + shift


def adaln(cond, w_mod, b_mod):
    m = jax.nn.silu(cond) @ w_mod + b_mod
    return jnp.split(m, N_MOD, axis=-1)


def swiglu(x, w_gate, w_up, w_down):
    return (jax.nn.silu(x @ w_gate) * (x @ w_up)) @ w_down


def ffn_sublayer(h, mods, pre_g, post_g, w_gate, w_up, w_down):
    shift, scale, gate = mods
    y = swiglu(modulate(rms_norm(h, pre_g), shift, scale), w_gate, w_up, w_down)
    return h + 0.5 * gate * rms_norm(y, post_g)


def axial_rope_tables(n, dtype):
    n_rows = n // GRID_W
    row = jnp.repeat(jnp.arange(n_rows, dtype=jnp.float32), GRID_W)
    col = jnp.tile(jnp.arange(GRID_W, dtype=jnp.float32), n_rows)
    inv_freq = ROPE_BASE ** (-2.0 * jnp.arange(N_FREQ, dtype=jnp.float32) / AXIS_DIM)
    ang = jnp.stack([row, col], axis=-1)[:, :, None] * inv_freq
    ang = jnp.stack([ang, ang], axis=-2).reshape(n, HEAD_DIM)
    return jnp.cos(ang).astype(dtype), jnp.sin(ang).astype(dtype)


def apply_rope(x, cos, sin):
    xs = x.reshape(x.shape[:-1] + (2, 2, N_FREQ))
    rot = jnp.stack([-xs[..., 1, :], xs[..., 0, :]], axis=-2).reshape(x.shape)
    return x * cos[None, :, None, None, :] + rot * sin[None, :, None, None, :]


def diff_attend(q, k, v, lam):
    s = jnp.einsum('bqhmd,bkhmd->bhmqk', q, k, preferred_element_type=jnp.float32) * (HEAD_DIM ** -0.5)
    p = jax.nn.softmax(s, axis=-1)
    a = p[:, :, 0] - lam * p[:, :, 1]
    return jnp.einsum('bhqk,bkhe->bqhe', a.astype(v.dtype), v)


def blocked_diff_attention(q, k, v, lam):
    b, n = q.shape[:2]
    nb = n // Q_BLOCK
    qb = q.reshape((b, nb, Q_BLOCK) + q.shape[2:]).swapaxes(0, 1)
    ob = lax.map(lambda qi: diff_attend(qi, k, v, lam), qb)
    return ob.swapaxes(0, 1).reshape(b, n, N_HEADS, V_DIM)


def short_conv(z, w):
    L = z.shape[1]
    pad = CONV_K // 2
    zp = jnp.pad(z, ((0, 0), (pad, pad), (0, 0)))
    return sum(zp[:, j:j + L] * w[j] for j in range(CONV_K))


def split_kv(u_kv):
    b, L = u_kv.shape[:2]
    k = u_kv[..., :QK_W].reshape(b, L, N_HEADS, 2, HEAD_DIM)
    v = u_kv[..., QK_W:].reshape(b, L, N_HEADS, V_DIM)
    return k, v


def diff_lambda(lq1, lk1, lq2, lk2, lam_init):
    f = jnp.float32
    return (jnp.exp(jnp.sum(lq1.astype(f) * lk1.astype(f)))
            - jnp.exp(jnp.sum(lq2.astype(f) * lk2.astype(f))) + lam_init)


def token_mixer(u, ext_k, ext_v, rope, lam, lam_init, subln_g, conv_w, w_pa, w_pb, w_o):
    b, L, _ = u.shape
    q = u[..., Q_OFF:K_OFF].reshape(b, L, N_HEADS, 2, HEAD_DIM)
    k, v = split_kv(u[..., K_OFF:B_OFF])
    if rope is not None:
        cos, sin = rope
        q = apply_rope(q, cos, sin)
        k = apply_rope(k, cos, sin)
    if ext_k is not None:
        k = jnp.concatenate([ext_k, k], axis=1)
        v = jnp.concatenate([ext_v, v], axis=1)
    o = blocked_diff_attention(q, k, v, lam)
    o = rms_norm(o, subln_g) * (1.0 - lam_init)
    y_att = o.reshape(b, L, ATT_W) @ w_pa
    gate_b = u[..., B_OFF:C_OFF]
    gate_c = u[..., C_OFF:X_OFF]
    x_in = u[..., X_OFF:GA_OFF]
    y_conv = (gate_b * short_conv(gate_c * x_in, conv_w)) @ w_pb
    merged = (jax.nn.sigmoid(u[..., GA_OFF:GB_OFF]) * y_att
              + jax.nn.sigmoid(u[..., GB_OFF:IN_W]) * y_conv)
    return merged @ w_o


def setup_inputs(seed: int = 0) -> dict:
    key = jax.random.key(seed)
    ks = jax.random.split(key, 32)
    L = DEPTH

    def nrm(k, shape, scale):
        return scale * jax.random.normal(k, shape, jnp.float32)

    def gain(k, shape):
        return 1.0 + 0.02 * jax.random.normal(k, shape, jnp.float32)

    d_s = D_MODEL ** -0.5
    f_s = D_FF ** -0.5
    return {
        "x": nrm(ks[0], (BATCH, SEQ, D_MODEL), 1.0),
        "c": nrm(ks[1], (BATCH, D_MODEL), 1.0),
        "ctx": nrm(ks[2], (BATCH, CTX_LEN, D_MODEL), 1.0),
        "c_ctx": nrm(ks[3], (D_MODEL,), 1.0),
        "w_mod": nrm(ks[4], (L, D_MODEL, N_MOD * D_MODEL), 0.5 * d_s),
        "b_mod": nrm(ks[5], (L, N_MOD * D_MODEL), 0.02),
        "ffn1_pre_g": gain(ks[6], (L, D_MODEL)),
        "ffn1_post_g": gain(ks[7], (L, D_MODEL)),
        "ffn1_w_gate": nrm(ks[8], (L, D_MODEL, D_FF), d_s),
        "ffn1_w_up": nrm(ks[9], (L, D_MODEL, D_FF), d_s),
        "ffn1_w_down": nrm(ks[10], (L, D_FF, D_MODEL), f_s),
        "mix_pre_g": gain(ks[11], (L, D_MODEL)),
        "mix_post_g": gain(ks[12], (L, D_MODEL)),
        "w_in": nrm(ks[13], (L, D_MODEL, IN_W), d_s),
        "lam_q1": nrm(ks[14], (L, HEAD_DIM), 0.1),
        "lam_k1": nrm(ks[15], (L, HEAD_DIM), 0.1),
        "lam_q2": nrm(ks[16], (L, HEAD_DIM), 0.1),
        "lam_k2": nrm(ks[17], (L, HEAD_DIM), 0.1),
        "attn_subln_g": gain(ks[18], (L, V_DIM)),
        "conv_w": nrm(ks[19], (L, CONV_K, CONV_W), CONV_K ** -0.5),
        "w_attn_proj": nrm(ks[20], (L, ATT_W, D_MODEL), ATT_W ** -0.5),
        "w_conv_proj": nrm(ks[21], (L, CONV_W, D_MODEL), CONV_W ** -0.5),
        "w_out": nrm(ks[22], (L, D_MODEL, D_MODEL), d_s),
        "ffn2_pre_g": gain(ks[23], (L, D_MODEL)),
        "ffn2_post_g": gain(ks[24], (L, D_MODEL)),
        "ffn2_w_gate": nrm(ks[25], (L, D_MODEL, D_FF), d_s),
        "ffn2_w_up": nrm(ks[26], (L, D_MODEL, D_FF), d_s),
        "ffn2_w_down": nrm(ks[27], (L, D_FF, D_MODEL), f_s),
    }


def reference(x, c, ctx, c_ctx, w_mod, b_mod,
              ffn1_pre_g, ffn1_post_g, ffn1_w_gate, ffn1_w_up, ffn1_w_down,
              mix_pre_g, mix_post_g, w_in, lam_q1, lam_k1, lam_q2, lam_k2,
              attn_subln_g, conv_w, w_attn_proj, w_conv_proj, w_out,
              ffn2_pre_g, ffn2_post_g, ffn2_w_gate, ffn2_w_up, ffn2_w_down):
    n = x.shape[1]
    rope = axial_rope_tables(n, x.dtype)
    h, hc = x, ctx
    for l in range(DEPTH):
        last = l == DEPTH - 1
        ml = [t[:, None, :] for t in adaln(c, w_mod[l], b_mod[l])]
        mc = adaln(c_ctx, w_mod[l], b_mod[l])

        ffn1 = (ffn1_pre_g[l], ffn1_post_g[l], ffn1_w_gate[l], ffn1_w_up[l], ffn1_w_down[l])
        h = ffn_sublayer(h, ml[0:3], *ffn1)
        hc = ffn_sublayer(hc, mc[0:3], *ffn1)

        lam_init = 0.8 - 0.6 * math.exp(-0.3 * l)
        lam = diff_lambda(lam_q1[l], lam_k1[l], lam_q2[l], lam_k2[l], lam_init)
        mix = (lam, lam_init, attn_subln_g[l], conv_w[l], w_attn_proj[l], w_conv_proj[l], w_out[l])
        xmc = modulate(rms_norm(hc, mix_pre_g[l]), mc[3], mc[4])
        if last:
            u_kv_c = xmc @ w_in[l][:, K_OFF:B_OFF]
        else:
            uc = xmc @ w_in[l]
            u_kv_c = uc[..., K_OFF:B_OFF]
            yc = token_mixer(uc, None, None, None, *mix)
            hc_mixed = hc + mc[5] * rms_norm(yc, mix_post_g[l])
        k_c, v_c = split_kv(u_kv_c)
        xm = modulate(rms_norm(h, mix_pre_g[l]), ml[3], ml[4])
        y = token_mixer(xm @ w_in[l], k_c, v_c, rope, *mix)
        h = h + ml[5] * rms_norm(y, mix_post_g[l])

        ffn2 = (ffn2_pre_g[l], ffn2_post_g[l], ffn2_w_gate[l], ffn2_w_up[l], ffn2_w_down[l])
        h = ffn_sublayer(h, ml[6:9], *ffn2)
        if not last:
            hc = ffn_sublayer(hc_mixed, mc[6:9], *ffn2)
    return h
```

```python
import math
from contextlib import ExitStack
import numpy as np
import concourse.bass as bass
import concourse.mybir as mybir
from concourse.bass_utils import run_bass_kernel_spmd

F32 = mybir.dt.float32
BF16 = mybir.dt.bfloat16
AF = mybir.ActivationFunctionType
ALU = mybir.AluOpType

P = 128
T = 2048
TB = 512
NB = 4
DC = 8
FC = 22
NCTX = 64
KW = T + NCTX
NKT = 17
EPS = 1e-6
LAM_INIT = 0.8 - 0.6 * math.exp(-0.3 * 0)
GROUPS = [[0, 1, 2, 3], [4, 5, 6, 7]]
SEC = {"q": 0, "qp": 1, "k": 2, "kp": 3, "b": 4, "c": 5, "x": 6, "ga": 7, "gb": 8}


class Buf:
    __slots__ = ("name", "excl", "w", "rd")

    def __init__(self, name, excl=False):
        self.name = name
        self.excl = excl
        self.w = None
        self.rd = []


class Sem:
    def __init__(self, h):
        self.h = h
        self.count = 0


class Op:
    __slots__ = ("eng", "fn", "deps", "signal", "val", "sem", "amt")

    def __init__(self, eng, fn):
        self.eng = eng
        self.fn = fn
        self.deps = ()
        self.signal = False
        self.val = 0
        self.sem = None
        self.amt = 1


class Prog:
    ENGS = ("pe", "act", "dve", "pool", "sp")

    def __init__(self):
        self.ops = {e: [] for e in self.ENGS}
        self.final = []

    def emit(self, eng, fn, r=(), w=(), sem=None, amt=16):
        op = Op(eng, fn)
        deps = set()
        for b in r:
            if b.w is not None:
                deps.add(b.w)
            if b.excl:
                for o in b.rd:
                    if o.eng != eng:
                        deps.add(o)
        for b in w:
            if b.w is not None:
                deps.add(b.w)
            deps.update(b.rd)
        if eng == "pe":
            deps = {d for d in deps if d.eng != "pe"}
        for d in deps:
            d.signal = True
        op.deps = deps
        for b in r:
            b.rd.append(op)
        for b in w:
            b.w = op
            b.rd = []
        if sem is not None:
            sem.count += amt
            op.sem, op.val, op.amt, op.signal = sem, sem.count, amt, True
        self.ops[eng].append(op)
        return op

    def finalize(self, esems):
        for e in self.ENGS:
            s = esems[e]
            for op in self.ops[e]:
                if op.sem is None and op.signal:
                    s.count += 1
                    op.sem, op.val, op.amt = s, s.count, 1

    def replay(self, eng, e):
        waited = {}
        for op in self.ops[eng]:
            need = {}
            for d in op.deps:
                if need.get(d.sem, 0) < d.val:
                    need[d.sem] = d.val
            for s, v in need.items():
                if waited.get(s, 0) < v:
                    e.wait_ge(s.h, v)
                    waited[s] = v
            ins = op.fn(e)
            if op.signal:
                ins.then_inc(op.sem.h, op.amt)
        if eng == "sp":
            for s in self.final:
                e.wait_ge(s.h, s.count)


def build(stage=3):
    nc = bass.Bass("TRN2", target_bir_lowering=False)
    es = ExitStack()

    def din(name, shape, dt=F32):
        return nc.dram_tensor(name, list(shape), dt, kind="ExternalInput")

    xT_d = din("xT", [1024, T])
    ctxT_d = din("ctxT", [1024, NCTX])
    cvec_d = din("cvec", [P, DC, 2])
    wmod_d = din("wmod", [72, P, DC * 128])
    bmod_d = din("bmod", [P, 72])
    gains_d = din("gains", [P, 6, DC])
    subg_d = din("subg", [P, 1])
    convw_d = din("convw", [P, 3, DC])
    lamv_d = din("lamv", [P, 4, 64])
    ropec_d = din("ropec", [P, T])
    ropes_d = din("ropes", [P, T])
    sel_d = din("sel", [P, 8])
    ffw_d = []
    for i in (1, 2):
        ffw_d.append((din(f"w{i}g", [FC, P, DC * 128]), din(f"w{i}u", [FC, P, DC * 128]),
                      din(f"w{i}d", [DC, P, FC * 128])))
    wins_d = din("wins", [72, P, DC * 128])
    winv_d = din("winv", [4, P, DC * 256])
    wpa_d = din("wpa", [DC, P, DC * 128])
    wpb_d = din("wpb", [DC, P, DC * 128])
    wo_d = din("wo", [DC, P, DC * 128])
    out_d = nc.dram_tensor("outT", [1024, T], F32, kind="ExternalOutput")

    kb_d = [nc.dram_tensor(f"kb{h}", [P, KW], BF16) for h in range(8)]
    kg_d = [nc.dram_tensor(f"kg{h}", [4 * P, KW], BF16) for h in range(8)]
    vb_d = [nc.dram_tensor(f"vb{h}", [P, NKT * 128], BF16) for h in range(8)]
    vg_d = [nc.dram_tensor(f"vg{h}", [4 * P, NKT * 128], BF16) for h in range(8)]
    hb_d = nc.dram_tensor("hb", [P, 16], BF16)
    hg_d = nc.dram_tensor("hg", [4 * P, 16], BF16)

    def sb(name, shape, dt):
        return es.enter_context(nc.sbuf_tensor(name, list(shape), dt))

    hT = sb("hT", [P, DC, T], F32)
    hc = sb("hc", [P, DC, NCTX], F32)
    xm = sb("xm", [P, DC, T + 2], BF16)
    xmc = sb("xmc", [P, DC, NCTX], BF16)
    arena = sb("arena", [P, 32 * 512], BF16)
    yT = sb("yT", [P, DC, TB], F32)
    NTMP = 4
    tmps = [sb(f"tmp{i}", [P, TB + 2], F32) for i in range(NTMP)]
    NWS = 4
    wsl = [sb(f"ws{i}", [P, DC, 128], BF16) for i in range(NWS)]
    wdl = [sb(f"wd{i}", [P, FC, 128], BF16) for i in range(2)]
    rstd_t = [sb(f"rstd{i}", [P, TB], F32) for i in range(1)]
    accs = [None, sb("acc1", [P, TB], F32)]
    ropeCs = [sb(f"ropeC{i}", [P, TB], F32) for i in range(2)]
    ropeSs = [sb(f"ropeS{i}", [P, TB], F32) for i in range(2)]
    cvec = sb("cvec_s", [P, DC, 2], F32)
    scv = sb("scv", [P, DC, 2], BF16)
    bmod = sb("bmod_s", [P, 72], F32)
    modv = sb("modv", [P, 72, 2], F32)
    gains = sb("gains_s", [P, 6, DC], F32)
    subg = sb("subg_s", [P, 1], F32)
    subg2 = sb("subg2", [P, 1], F32)
    convw = sb("convw_s", [P, 3, DC], F32)
    lamv = sb("lamv_s", [P, 4, 64], F32)
    lamt = sb("lamt", [P, 2, 64], F32)
    lams = sb("lams", [P, 4], F32)
    nlam = sb("nlam", [P, 1], F32)
    sel = sb("sel_s", [P, 8], F32)
    coef = sb("coef", [P, 2, 9, DC], F32)
    ones_b = sb("ones_b", [P, P], BF16)
    ones_f = sb("ones_f", [P, P], F32)
    hgs = sb("hgs", [P, 4, 16], BF16)
    hst = sb("hst", [P, 2, DC], BF16)
    eps_t = sb("eps_t", [P, 1], F32)
    hacc = sb("hacc", [P, 2, DC], F32)
    zedge = sb("zedge", [P, DC, 2, 2], F32)
    zedge2 = sb("zedge2", [P, DC, 2], F32)
    psum = [es.enter_context(nc.psum_tensor(f"ps{i}", [P, TB], F32)) for i in range(8)]

    def mksem(name):
        return Sem(es.enter_context(nc.semaphore(name)))

    esems = {e: mksem("e_" + e) for e in Prog.ENGS}
    pg = Prog()

    B_h = [[Buf(f"h{c}_{b}") for b in range(NB)] for c in range(DC)]
    B_hc = Buf("hc")
    B_xm = [Buf(f"xm{b}") for b in range(NB)]
    B_xmL, B_xmR = Buf("xmL"), Buf("xmR")
    B_xmc = Buf("xmc")
    B_pg = [Buf(f"pg{i}") for i in range(32)]
    B_y = [Buf(f"y{c}") for c in range(DC)]
    B_tmp = [Buf(f"tmp{i}") for i in range(NTMP)]
    B_ws = [Buf(f"ws{i}") for i in range(NWS)]
    B_wd = [Buf(f"wd{i}") for i in range(2)]
    B_ps = [Buf(f"ps{i}", excl=True) for i in range(8)]
    B_ropeCs = [Buf("ropeC0"), Buf("ropeC1")]
    B_ropeSs = [Buf("ropeS0"), Buf("ropeS1")]
    B_ze, B_ze2, B_hacc, B_hst = Buf("ze"), Buf("ze2"), Buf("hacc"), Buf("hst")
    B_rstd = [Buf(f"rstd{i}") for i in range(2)]
    B_acc = [Buf(f"acc{i}") for i in range(2)]
    B_const = Buf("const")
    B_modv = Buf("modv")
    B_coef = Buf("coef")
    B_misc = Buf("misc")
    B_kb = [Buf(f"kb{h}") for h in range(8)]
    B_kg = [Buf(f"kg{h}") for h in range(8)]
    B_vb = [Buf(f"vb{h}") for h in range(8)]
    B_vg = [Buf(f"vg{h}") for h in range(8)]
    B_hb, B_hg, B_hgs = Buf("hb"), Buf("hg"), Buf("hgs")
    B_out = [Buf(f"out{b}") for b in range(NB)]

    S_ws = [mksem(f"s_ws{i}") for i in range(NWS)]
    S_wd = [mksem(f"s_wd{i}") for i in range(2)]
    S_x = [mksem(f"s_x{i}") for i in range(NB)]
    S_const = mksem("s_const")
    S_ropeCs = [mksem("s_ropeC0"), mksem("s_ropeC1")]
    S_ropeSs = [mksem("s_ropeS0"), mksem("s_ropeS1")]
    S_kc = [mksem(f"s_kc{i}") for i in range(2)]
    S_vc = [mksem(f"s_vc{i}") for i in range(2)]
    S_kb = [mksem(f"s_kb{h}") for h in range(8)]
    S_vb = [mksem(f"s_vb{h}") for h in range(8)]
    S_kg = [mksem(f"s_kg{h}") for h in range(8)]
    S_vg = [mksem(f"s_vg{h}") for h in range(8)]
    S_hb, S_hg, S_hgs = mksem("s_hb"), mksem("s_hg"), mksem("s_hgs")
    S_out = mksem("s_out")
    S_wv = [mksem(f"s_wv{i}") for i in range(2)]

    rot = {"tmp": 0, "ws": 0, "wd": 0, "rstd": 0, "ps": 0}

    def tmp():
        i = rot["tmp"]
        rot["tmp"] = (i + 1) % NTMP
        return tmps[i], B_tmp[i]

    def page(i, n=1):
        return arena[:, i * 512:(i + n) * 512]

    def mm(out, lhsT, rhs, start, stop, r, w):
        pg.emit("pe", lambda e, o=out, l=lhsT, rr=rhs, s=start, t=stop: e.matmul(o, lhsT=l, rhs=rr, start=s, stop=t),
                r=r, w=w)

    def act(out, in_, func, r, w, scale=None, bias=None):
        kw = {}
        if scale is not None:
            kw["scale"] = scale
        if bias is not None:
            kw["bias"] = bias
        pg.emit("act", lambda e, o=out, i=in_, f=func, k=kw: e.activation(out=o, in_=i, func=f, **k), r=r, w=w)

    def tt(out, in0, in1, op, r, w, eng="dve"):
        pg.emit(eng, lambda e, o=out, a=in0, b=in1, p=op: e.tensor_tensor(out=o, in0=a, in1=b, op=p), r=r, w=w)

    def ts(out, in0, s1, op0, r, w, s2=None, op1=None, eng="dve"):
        if op1 is None:
            pg.emit(eng, lambda e, o=out, a=in0, s=s1, p=op0: e.tensor_scalar(out=o, in0=a, scalar1=s, scalar2=None, op0=p),
                    r=r, w=w)
        else:
            pg.emit(eng, lambda e, o=out, a=in0, s=s1, p=op0, ss=s2, pp=op1:
                    e.tensor_scalar(out=o, in0=a, scalar1=s, scalar2=ss, op0=p, op1=pp), r=r, w=w)

    def stt(out, in0, scalar, in1, op0, op1, r, w):
        pg.emit("dve", lambda e, o=out, a=in0, s=scalar, b=in1, p0=op0, p1=op1:
                e.scalar_tensor_tensor(out=o, in0=a, scalar=s, in1=b, op0=p0, op1=p1), r=r, w=w)

    def dma(q, out, in_, r, w, sem):
        return pg.emit(q, lambda e, o=out, i=in_: e.dma_start(out=o, in_=i), r=r, w=w, sem=sem, amt=16)

    ws_slots = [(wsl[i][:].rearrange("p c n -> p (c n)"), wsl[i], [B_ws[i]], S_ws[i]) for i in range(NWS)]
    ws_state = {"slots": ws_slots, "i": 0}

    def load_ws(src_ap):
        sl = ws_state["slots"]
        i = ws_state["i"] % len(sl)
        ws_state["i"] = i + 1
        flat, w3, bufs, sem = sl[i]
        dma("pool", flat, src_ap, r=[], w=bufs, sem=sem)
        return w3, bufs

    def set_ws(extra=()):
        ws_state["slots"] = ws_slots + list(extra)
        ws_state["i"] = 0

    for dst, src in ((cvec, cvec_d), (bmod, bmod_d), (gains, gains_d), (subg, subg_d), (convw, convw_d),
                     (lamv, lamv_d), (sel, sel_d)):
        dma("sp", dst[:], src.ap(), r=[], w=[B_const], sem=S_const)
    for b in range(NB):
        dma("sp", hT[:, :, b * TB:(b + 1) * TB],
            xT_d.ap().rearrange("(c p) t -> p c t", p=P)[:, :, b * TB:(b + 1) * TB],
            r=[], w=[B_h[c][b] for c in range(DC)], sem=S_x[b])
    dma("sp", hc[:], ctxT_d.ap().rearrange("(c p) t -> p c t", p=P), r=[], w=[B_hc], sem=S_const)
    pg.emit("dve", lambda e: e.memset(ones_b[:], 1.0), r=[], w=[B_misc])
    pg.emit("dve", lambda e: e.memset(ones_f[:], 1.0), r=[B_misc], w=[B_misc])
    pg.emit("dve", lambda e: e.memset(eps_t[:], EPS), r=[B_misc], w=[B_misc])

    def rstd_tile():
        i = rot["rstd"]
        rot["rstd"] = 0
        return rstd_t[i], B_rstd[i]

    AX = mybir.AxisListType.X
    MUL, ADD, SUB = ALU.mult, ALU.add, ALU.subtract

    act(scv[:], cvec[:], AF.Silu, r=[B_const], w=[B_misc])
    B_modvs = [Buf(f"modv{i}") for i in range(3)]
    B_coefs = [Buf(f"coef{i}") for i in range(3)]

    def mod_part_steps(s_, bank):
        j0 = 24 * s_
        steps = []

        def panel(j):
            bk = bank + (j % 2)
            wt, wb = load_ws(wmod_d[j, :, :])
            for c in range(DC):
                mm(psum[bk][:, 0:2], wt[:, c, :], scv[:, c, :], c == 0, c == DC - 1, wb + [B_misc], [B_ps[bk]])
            ts(modv[:, j, :], psum[bk][:, 0:2], bmod[:, j:j + 1], ADD, r=[B_ps[bk], B_const], w=[B_modvs[s_]])

        def final():
            half = 0.5 if s_ != 1 else 1.0
            for v in range(2):
                stt(coef[:, v, 3 * s_, :], modv[:, (3 * s_ + 1) * 8:(3 * s_ + 2) * 8, v], 1.0, gains[:, 2 * s_, :], ADD, MUL,
                    r=[B_modvs[s_], B_const], w=[B_coefs[s_]])
                ts(coef[:, v, 3 * s_ + 1, :], modv[:, (3 * s_) * 8:(3 * s_ + 1) * 8, v], 1.0, MUL, r=[B_modvs[s_]], w=[B_coefs[s_]])
                stt(coef[:, v, 3 * s_ + 2, :], modv[:, (3 * s_ + 2) * 8:(3 * s_ + 3) * 8, v], half, gains[:, 2 * s_ + 1, :], MUL, MUL,
                    r=[B_modvs[s_], B_const], w=[B_coefs[s_]])

        for j in range(j0, j0 + 24):
            steps.append(lambda j=j: panel(j))
        steps.append(final)
        return steps

    def mod_part(s_, bank):
        for st_ in mod_part_steps(s_, bank):
            st_()

    pending_mod = []

    mod_part(0, 0)
    tt(lamt[:, 0, :], lamv[:, 0, :], lamv[:, 1, :], MUL, r=[B_const], w=[B_misc])
    tt(lamt[:, 1, :], lamv[:, 2, :], lamv[:, 3, :], MUL, r=[B_const, B_misc], w=[B_misc])
    pg.emit("dve", lambda e: e.tensor_reduce(out=lams[:, 0:2], in_=lamt[:], axis=AX, op=ADD), r=[B_misc], w=[B_misc])
    act(lams[:, 2:4], lams[:, 0:2], AF.Exp, r=[B_misc], w=[B_misc])
    stt(nlam[:], lams[:, 3:4], -LAM_INIT, lams[:, 2:3], ADD, SUB, r=[B_misc], w=[B_misc])
    ts(subg2[:], subg[:], 1.0 - LAM_INIT, MUL, r=[B_const, B_misc], w=[B_misc])

    class Blk:
        def __init__(self, idx):
            self.idx = idx
            if idx >= 0:
                self.n, self.v = TB, 0
                t0 = idx * TB
                self.h = lambda c: hT[:, c, t0:t0 + TB]
                self.hb = lambda c: B_h[c][idx]
            else:
                self.n, self.v = NCTX, 1
                self.h = lambda c: hc[:, c, :]
                self.hb = lambda c: B_hc

    def stats_rstd(bank, n, nfeat):
        rs, rsb = rstd_tile()
        act(rs[:, 0:n], psum[bank][:, 0:n], AF.Ln, r=[B_ps[bank]], w=[rsb], scale=1.0 / nfeat, bias=eps_t[:, 0:1])
        act(rs[:, 0:n], rs[:, 0:n], AF.Exp, r=[rsb], w=[rsb], scale=-0.5)
        return rs, rsb

    xq_bufs = {}

    def prenorm(blk, kA, xh, xb):
        n, v = blk.n, blk.v
        for c in range(DC):
            qb = xq_bufs.setdefault((id(xb), c), Buf("xq"))
            if c == 0:
                act(xh(c), blk.h(c), AF.Square, r=[blk.hb(c)], w=[xb, qb])
            else:
                act(xh(c), blk.h(c), AF.Square, r=[blk.hb(c), xb], w=[qb])
            mm(psum[6][:, 0:n], ones_b[:], xh(c), c == 0, c == DC - 1, [B_misc, qb], [B_ps[6]])
        rs, rsb = stats_rstd(6, n, 1024.0)
        for c in range(DC):
            qb = xq_bufs[(id(xb), c)]
            t, tb = tmp()
            stt(t[:, 0:n], blk.h(c), coef[:, v, kA, c:c + 1], rs[:, 0:n], MUL, MUL, r=[blk.hb(c), B_coefs[kA // 3], rsb], w=[tb])
            act(xh(c), t[:, 0:n], AF.Identity, r=[tb, B_coefs[kA // 3]], w=[xb, qb], bias=coef[:, v, kA + 1, c:c + 1])

    def postnorm_update(blk, kB):
        n, v = blk.n, blk.v
        rs, rsb = stats_rstd(7, n, 1024.0)
        for c in range(DC):
            t, tb = tmp()
            stt(t[:, 0:n], yT[:, c, 0:n], coef[:, v, kB, c:c + 1], rs[:, 0:n], MUL, MUL, r=[B_y[c], B_coefs[kB // 3], rsb], w=[tb])
            tt(blk.h(c), blk.h(c), t[:, 0:n], ADD, r=[blk.hb(c), tb], w=[blk.hb(c)], eng="dve")

    def y_evac(bank, c, n):
        pg.emit("dve", lambda e, o=yT[:, c, 0:n], i=psum[bank][:, 0:n]: e.tensor_copy(out=o, in_=i), r=[B_ps[bank]], w=[B_y[c]])
        t, tb = tmp()
        act(t[:, 0:n], yT[:, c, 0:n], AF.Square, r=[B_y[c]], w=[tb])
        return lambda: mm(psum[7][:, 0:n], ones_f[:], t[:, 0:n], c == 0, c == DC - 1, [B_misc, tb], [B_ps[7]])

    B_pgx = [Buf(f"pgx{i}") for i in range(16)]
    bank_ctr = [0]

    def act_page(f, si, n, is_ctx):
        if is_ctx:
            idx, c0 = 44 + f // 8, (f % 8) * NCTX
        else:
            idx, c0 = 2 * f + si, 0
        if idx < 32:
            return arena[:, idx * 512 + c0:idx * 512 + c0 + n], [B_pg[idx]]
        i = idx - 32
        col = 1025 + (i // 8) * 512 + c0
        return xm[:, i % 8, col:col + n], [B_pgx[i], B_xm[2 + i // 8]]

    def xhat_of(blk, si):
        if blk.idx >= 0:
            return (lambda c: xm[:, c, 1 + si * TB:1 + (si + 1) * TB]), B_xm[si]
        return (lambda c: xmc[:, c, :]), B_xmc

    def ffn_super(blocks, wg_d, wu_d, wd_d, kB, between=None):
        for f in range(FC):
            wg, wgb = load_ws(wg_d[f, :, :])
            wu, wub = load_ws(wu_d[f, :, :])
            for si, blk in enumerate(blocks):
                n = blk.n
                xh, xb = xhat_of(blk, si)
                k = bank_ctr[0] % 2
                bank_ctr[0] += 1
                gb_, ub_ = k, 2 + k
                for c in range(DC):
                    mm(psum[gb_][:, 0:n], wg[:, c, :], xh(c), c == 0, c == DC - 1, wgb + [xb], [B_ps[gb_]])
                for c in range(DC):
                    mm(psum[ub_][:, 0:n], wu[:, c, :], xh(c), c == 0, c == DC - 1, wub + [xb], [B_ps[ub_]])
                t, tb = tmp()
                act(t[:, 0:n], psum[gb_][:, 0:n], AF.Silu, r=[B_ps[gb_]], w=[tb])
                ap_, bufs_ = act_page(f, si, n, blk.idx < 0)
                tt(ap_, psum[ub_][:, 0:n], t[:, 0:n], MUL, r=[B_ps[ub_], tb], w=bufs_)
        if between is not None:
            between()
        for si, blk in enumerate(blocks):
            n = blk.n
            pend = None
            for co in range(DC):
                i = rot["wd"]
                rot["wd"] = (i + 1) % 2
                dma("pool", wdl[i][:].rearrange("p f n -> p (f n)"), wd_d[co, :, :], r=[], w=[B_wd[i]], sem=S_wd[i])
                bank = 4 + (co % 2)
                for f in range(FC):
                    ap_, bufs_ = act_page(f, si, n, blk.idx < 0)
                    mm(psum[bank][:, 0:n], wdl[i][:, f, :], ap_, f == 0, f == FC - 1, [B_wd[i]] + bufs_, [B_ps[bank]])
                if pend is not None:
                    pend()
                pend = y_evac(bank, co, n)
                for _ in range(2):
                    if pending_mod:
                        pending_mod.pop(0)()
            pend()
            postnorm_update(blk, kB)

    def ffn_layer(supers, widx, kA, kB, after_first=None):
        wg_d, wu_d, wd_d = ffw_d[widx]

        def prenorm_super(blocks):
            for si, blk in enumerate(blocks):
                xh, xb = xhat_of(blk, si)
                prenorm(blk, kA, xh, xb)

        prenorm_super(supers[0])
        for i, blocks in enumerate(supers):
            def between(i=i):
                if i == 0 and after_first is not None:
                    after_first()
                if i + 1 < len(supers):
                    prenorm_super(supers[i + 1])
            ffn_super(blocks, wg_d, wu_d, wd_d, kB, between)

    lat = [Blk(b) for b in range(NB)]
    ctxb = Blk(-1)

    pending_mod.extend(mod_part_steps(1, 2) + mod_part_steps(2, 2))
    ffn_layer([[lat[0], lat[1], ctxb], [lat[2], lat[3]]], 0, 0, 2)
    while pending_mod:
        pending_mod.pop(0)()

    if stage >= 2:
        for b in range(NB):
            prenorm(lat[b], 3, (lambda c, b=b: xm[:, c, 1 + b * TB:1 + (b + 1) * TB]), B_xm[b])
        prenorm(ctxb, 3, (lambda c: xmc[:, c, :]), B_xmc)

        pg.emit("dve", lambda e: e.tensor_copy(out=hst[:, 0, :], in_=xm[:, :, 1]), r=[B_xm[0]], w=[B_hst])
        pg.emit("dve", lambda e: e.tensor_copy(out=hst[:, 1, :], in_=xm[:, :, T]), r=[B_xm[NB - 1], B_hst], w=[B_hst])
        dma("sp", hb_d.ap(), hst[:].rearrange("p s c -> p (s c)"), r=[B_hst], w=[B_hb], sem=S_hb)
        pg.emit("pool", lambda e: e.collective_compute("AllGather", ALU.bypass, replica_groups=GROUPS,
                                                       ins=[hb_d.ap().opt()], outs=[hg_d.ap().opt()]),
                r=[B_hb], w=[B_hg], sem=S_hg, amt=1)
        dma("sp", hgs[:], hg_d.ap().rearrange("(r p) n -> p r n", p=P), r=[B_hg], w=[B_hgs], sem=S_hgs)
        hgv = hgs[:].rearrange("p r (s c) -> p r s c", s=2)
        for side in range(2):
            src_s = 1 - side
            ts(hacc[:, side, :], hgv[:, 0, src_s, :], sel[:, side * 4:side * 4 + 1], MUL, r=[B_hgs, B_const], w=[B_hacc])
            for r_ in range(1, 4):
                stt(hacc[:, side, :], hgv[:, r_, src_s, :], sel[:, side * 4 + r_:side * 4 + r_ + 1], hacc[:, side, :], MUL, ADD,
                    r=[B_hgs, B_const, B_hacc], w=[B_hacc])
        pg.emit("dve", lambda e: e.tensor_copy(out=xm[:, :, 0], in_=hacc[:, 0, :]), r=[B_hacc], w=[B_xmL])
        pg.emit("dve", lambda e: e.tensor_copy(out=xm[:, :, T + 1], in_=hacc[:, 1, :]), r=[B_hacc], w=[B_xmR])

        rope_ctr = [0]
        ropeC_v = [ropeCs[0][:], ropeCs[1][:], page(23, 2).bitcast(F32), page(27, 2).bitcast(F32)]
        ropeS_v = [ropeSs[0][:], ropeSs[1][:], page(25, 2).bitcast(F32), page(29, 2).bitcast(F32)]
        B_ropeC_v = [[B_ropeCs[0]], [B_ropeCs[1]], B_pg[23:25], B_pg[27:29]]
        B_ropeS_v = [[B_ropeSs[0]], [B_ropeSs[1]], B_pg[25:27], B_pg[29:31]]
        S_rope_x = [mksem(f"s_ropex{i}") for i in range(4)]

        def load_rope(b, ri=None):
            if ri is None:
                ri = rope_ctr[0] % 2
                rope_ctr[0] += 1
            if ri < 2:
                sc_, ss_ = S_ropeCs[ri], S_ropeSs[ri]
            else:
                sc_, ss_ = S_rope_x[(ri - 2) * 2], S_rope_x[(ri - 2) * 2 + 1]
            dma("sp", ropeC_v[ri], ropec_d[:, b * TB:(b + 1) * TB], r=[], w=B_ropeC_v[ri], sem=sc_)
            dma("sp", ropeS_v[ri], ropes_d[:, b * TB:(b + 1) * TB], r=[], w=B_ropeS_v[ri], sem=ss_)
            return ri

        def rope_evac(bk_a, bk_p, out_ap, out_bufs, ri):
            t1, t1b = tmp()
            t2, t2b = tmp()
            tt(t1[:, 0:TB], psum[bk_a][:], ropeC_v[ri], MUL, r=[B_ps[bk_a]] + B_ropeC_v[ri], w=[t1b])
            tt(t2[:, 0:TB], psum[bk_p][:], ropeS_v[ri], MUL, r=[B_ps[bk_p]] + B_ropeS_v[ri], w=[t2b])
            tt(out_ap, t1[:, 0:TB], t2[:, 0:TB], ADD, r=[t1b, t2b], w=out_bufs, eng="dve")

        wvs = page(0, 4).rearrange("p (c n) -> p c n", c=DC)
        B_wvs = B_pg[0:4]
        vst = arena[:, 4 * 512:4 * 512 + 2 * NKT * 128].rearrange("p (h j e) -> p h j e", h=2, j=NKT)
        B_vst = B_pg[4:13]
        kst = [page(13, 5), page(18, 5)]
        B_kst = [B_pg[13:18], B_pg[18:23]]

        pending_cc = []

        def flush_cc():
            for fn_ in pending_cc:
                fn_()
            del pending_cc[:]

        def v_pair(hp):
            dma("pool", wvs.rearrange("p c n -> p (c n)"), winv_d[hp, :, :], r=[], w=B_wvs, sem=S_wv[0])
            for j in range(NKT):
                bank = j % 2
                nk = 128 if j < 16 else NCTX
                for c in range(DC):
                    lhsT = xm[:, c, 1 + j * 128:1 + (j + 1) * 128] if j < 16 else xmc[:, c, :]
                    rb = B_xm[j // 4] if j < 16 else B_xmc
                    mm(psum[bank][0:nk, 0:256], lhsT, wvs[:, c, :], c == 0, c == DC - 1, [rb] + B_wvs, [B_ps[bank]])
                act(vst[0:nk, :, j, :], psum[bank][0:nk, 0:256].rearrange("p (h e) -> p h e", h=2), AF.Identity,
                    r=[B_ps[bank]], w=B_vst)
            for hh in range(2):
                h = hp * 2 + hh
                dma("sp", vb_d[h].ap(), vst[:, hh, :, :].rearrange("p j e -> p (j e)"), r=B_vst, w=[B_vb[h]], sem=S_vb[h])
                pending_cc.append(lambda h=h: pg.emit("pool", lambda e, h=h: e.collective_compute(
                    "AllGather", ALU.bypass, replica_groups=GROUPS, ins=[vb_d[h].ap().opt()], outs=[vg_d[h].ap().opt()], dma_qos="P2"),
                    r=[B_vb[h]], w=[B_vg[h]], sem=S_vg[h], amt=1))

        def k_head(h):
            si = h % 2
            ks, ksb = kst[si], B_kst[si]
            wk, wkb = load_ws(wins_d[SEC["k"] * 8 + h, :, :])
            wp, wpb_ = load_ws(wins_d[SEC["kp"] * 8 + h, :, :])
            for b in range(NB):
                ri = b
                rhs_b = B_xm[b]
                ka, kp_ = (2, 3) if b % 2 == 0 else (4, 5)
                for c in range(DC):
                    mm(psum[ka][:], wk[:, c, :], xm[:, c, 1 + b * TB:1 + (b + 1) * TB], c == 0, c == DC - 1, wkb + [rhs_b], [B_ps[ka]])
                for c in range(DC):
                    mm(psum[kp_][:], wp[:, c, :], xm[:, c, 1 + b * TB:1 + (b + 1) * TB], c == 0, c == DC - 1, wpb_ + [rhs_b], [B_ps[kp_]])
                kb2, kb3 = (2, 3) if b % 2 == 0 else (4, 5)
                rope_evac(kb2, kb3, ks[:, b * TB:(b + 1) * TB], ksb, ri)
            for c in range(DC):
                mm(psum[2][:, 0:NCTX], wk[:, c, :], xmc[:, c, :], c == 0, c == DC - 1, wkb + [B_xmc], [B_ps[2]])
            act(ks[:, T:T + NCTX], psum[2][:, 0:NCTX], AF.Identity, r=[B_ps[2]], w=ksb)
            dma("sp", kb_d[h].ap(), ks[:, 0:KW], r=ksb, w=[B_kb[h]], sem=S_kb[h])
            flush_cc()
            pending_cc.append(lambda h=h: pg.emit("pool", lambda e, h=h: e.collective_compute(
                "AllGather", ALU.bypass, replica_groups=GROUPS, ins=[kb_d[h].ap().opt()], outs=[kg_d[h].ap().opt()], dma_qos="P2"),
                r=[B_kb[h]], w=[B_kg[h]], sem=S_kg[h], amt=1))

        for b in range(NB):
            load_rope(b, b)
        for hp in range(4):
            v_pair(hp)
            k_head(2 * hp)
            k_head(2 * hp + 1)
        flush_cc()

        qT = [page(h) for h in range(8)]
        B_q = B_pg[0:8]
        onT = [page(8 + h) for h in range(8)]
        B_on = B_pg[8:16]
        pT = [page(16 + i) for i in range(4)]
        B_pT = B_pg[16:20]
        kc = [page(20, 3), page(23, 3)]
        B_kc = [B_pg[20:23], B_pg[23:26]]
        vc = [page(26, 3), page(29, 3)]
        B_vc = [B_pg[26:29], B_pg[29:32]]
        cgT = [page(16 + c) for c in range(8)]
        B_cg = B_pg[16:24]
        mrg = [page(24 + c) for c in range(8)]
        B_mrg = B_pg[24:32]
        chunk_ctr = [0]
        S_wsxA = [mksem(f"s_wsxa{i}") for i in range(4)]
        S_wsxB = [mksem(f"s_wsxb{i}") for i in range(4)]

        def page_slots(p0, sems):
            out = []
            for k in range(4):
                flat = page(p0 + 2 * k, 2)
                out.append((flat, flat.rearrange("p (c n) -> p c n", c=DC), B_pg[p0 + 2 * k:p0 + 2 * k + 2], sems[k]))
            return out

        extraA = page_slots(24, S_wsxA)
        extraB = page_slots(16, S_wsxB)

        def proj(sec, co, b, bank, edge=False):
            w, wb = load_ws(wins_d[SEC[sec] * 8 + co, :, :])
            for c in range(DC):
                mm(psum[bank][:], w[:, c, :], xm[:, c, 1 + b * TB:1 + (b + 1) * TB], c == 0, c == DC - 1, wb + [B_xm[b]], [B_ps[bank]])
            return w, wb

        def attention_block(b):
            set_ws(extraA)
            ri = load_rope(b)
            for h in range(8):
                qa, qb_ = (0, 1) if h % 2 == 0 else (2, 3)
                proj("q", h, b, qa)
                proj("qp", h, b, qb_)
                rope_evac(qa, qb_, qT[h], [B_q[h]], ri)
            conv_branch(b)
            deferred = []
            for h in range(8):
                tiles = []
                chunks = []
                for r_ in range(4):
                    for half in range(2):
                        s_ = chunk_ctr[0] % 2
                        chunk_ctr[0] += 1
                        if half == 0:
                            ncol, j0, nj = 1024, 0, 8
                        else:
                            ncol, j0, nj = 1024 + NCTX, 8, 9
                        chunks.append((s_, r_, half * 1024, ncol, j0, nj))
                        for jj in range(nj):
                            nk = NCTX if (half == 1 and jj == nj - 1) else 128
                            tiles.append((s_, jj, nk, len(chunks) - 1, jj == 0))

                def load_chunk(k):
                    s_, r_, c0, ncol, j0, nj = chunks[k]
                    dma("sp", kc[s_][:, 0:ncol], kg_d[h][r_ * P:(r_ + 1) * P, c0:c0 + ncol], r=[B_kg[h]], w=B_kc[s_], sem=S_kc[s_])
                    dma("sp", vc[s_][:, 0:nj * 128], vg_d[h][r_ * P:(r_ + 1) * P, j0 * 128:(j0 + nj) * 128],
                        r=[B_vg[h]], w=B_vc[s_], sem=S_vc[s_])

                load_chunk(0)
                nt = len(tiles)

                def s_mm(i):
                    s_, jj, nk, _, _ = tiles[i]
                    ba, bb = (i % 2) * 2, (i % 2) * 2 + 1
                    mm(psum[ba][0:nk, :], kc[s_][0:64, jj * 128:jj * 128 + nk], qT[h][0:64, :], True, True, B_kc[s_] + [B_q[h]], [B_ps[ba]])
                    mm(psum[bb][0:nk, :], kc[s_][64:128, jj * 128:jj * 128 + nk], qT[h][64:128, :], True, True, B_kc[s_] + [B_q[h]], [B_ps[bb]])

                s_mm(0)
                for i in range(nt):
                    s_, jj, nk, ck, fst = tiles[i]
                    if fst and ck + 1 < len(chunks):
                        load_chunk(ck + 1)
                    ba, bb = (i % 2) * 2, (i % 2) * 2 + 1
                    pa, pb = (i % 2) * 2, (i % 2) * 2 + 1
                    act(pT[pa][0:nk, :], psum[ba][0:nk, :], AF.Exp, r=[B_ps[ba]], w=[B_pT[pa]], scale=0.125)
                    act(pT[pb][0:nk, :], psum[bb][0:nk, :], AF.Exp, r=[B_ps[bb]], w=[B_pT[pb]], scale=0.125)
                    if i + 1 < nt:
                        s_mm(i + 1)
                    first, last = i == 0, i == nt - 1
                    vt = vc[s_][0:nk, jj * 128:(jj + 1) * 128]
                    mm(psum[4][:], vt, pT[pa][0:nk, :], first, last, B_vc[s_] + [B_pT[pa]], [B_ps[4]])
                    mm(psum[5][:], vt, pT[pb][0:nk, :], first, last, B_vc[s_] + [B_pT[pb]], [B_ps[5]])
                    mm(psum[6][:], ones_b[0:nk, :], pT[pa][0:nk, :], first, last, [B_misc, B_pT[pa]], [B_ps[6]])
                    if first:
                        pg.emit("dve", lambda e, o=accs[1][:], i_=pT[pb][:]: e.tensor_copy(out=o, in_=i_), r=[B_pT[pb]], w=[B_acc[1]])
                    else:
                        tt(accs[1][0:nk, :], accs[1][0:nk, :], pT[pb][0:nk, :], ADD, r=[B_acc[1], B_pT[pb]], w=[B_acc[1]])
                    if deferred and i in (1, 3, 5, 7, 9, 11):
                        deferred.pop(0)()
                mm(psum[7][:], ones_f[:], accs[1][:], True, True, [B_misc, B_acc[1]], [B_ps[7]])
                r0, r0b = tmp()
                act(r0[:, 0:TB], psum[6][:], AF.Ln, r=[B_ps[6]], w=[r0b])
                o0, o0b = tmp()
                o1, o1b = tmp()
                pg.emit("dve", lambda e, o=o0[:, 0:TB], i_=psum[4][:]: e.tensor_copy(out=o, in_=i_), r=[B_ps[4]], w=[o0b])
                pg.emit("dve", lambda e, o=o1[:, 0:TB], i_=psum[5][:]: e.tensor_copy(out=o, in_=i_), r=[B_ps[5]], w=[o1b])
                r1, r1b = tmp()
                deferred.extend(norm_groups(h, r0, r0b, r1, r1b, o0, o0b, o1, o1b))
            while deferred:
                deferred.pop(0)()

        def norm_groups(h, r0, r0b, r1, r1b, o0, o0b, o1, o1b):
            st = {}

            def g1():
                act(r0[:, 0:TB], r0[:, 0:TB], AF.Exp, r=[r0b], w=[r0b], scale=-1.0)
                act(r1[:, 0:TB], psum[7][:], AF.Ln, r=[B_ps[7]], w=[r1b])
                act(r1[:, 0:TB], r1[:, 0:TB], AF.Exp, r=[r1b], w=[r1b], scale=-1.0)

            def g2():
                tt(o0[:, 0:TB], o0[:, 0:TB], r0[:, 0:TB], MUL, r=[o0b, r0b], w=[o0b])
                tt(o1[:, 0:TB], o1[:, 0:TB], r1[:, 0:TB], MUL, r=[o1b, r1b], w=[o1b])
                stt(o0[:, 0:TB], o1[:, 0:TB], nlam[:, 0:1], o0[:, 0:TB], MUL, ADD, r=[o1b, o0b, B_misc], w=[o0b])

            def g3():
                st["sq"] = tmp()
                sq, sqb = st["sq"]
                act(sq[:, 0:TB], o0[:, 0:TB], AF.Square, r=[o0b], w=[sqb])

            def g4():
                sq, sqb = st["sq"]
                mm(psum[7][:], ones_f[:], sq[:, 0:TB], True, True, [B_misc, sqb], [B_ps[7]])

            def g5():
                st["rs"] = stats_rstd(7, TB, 128.0)

            def g6():
                rs, rsb = st["rs"]
                stt(onT[h], o0[:, 0:TB], subg2[:, 0:1], rs[:, 0:TB], MUL, MUL, r=[o0b, B_misc, rsb], w=[B_on[h]])

            return [g1, g2, g3, g4, g5, g6]

        def conv_branch(b):
            lb = B_xm[b - 1] if b > 0 else B_xmL
            rb = B_xm[b + 1] if b + 1 < NB else B_xmR
            c_l, c_r = b * TB, b * TB + TB + 1
            for cc in range(DC):
                for si, sec in enumerate(("c", "x")):
                    w, wb = load_ws(wins_d[SEC[sec] * 8 + cc, :, :])
                    o = (cc * 2 + si) * 2
                    for c in range(DC):
                        mm(psum[4][:, o:o + 1], w[:, c, :], xm[:, c, c_l:c_l + 1], c == 0, c == DC - 1, wb + [lb], [B_ps[4]])
                    for c in range(DC):
                        mm(psum[4][:, o + 1:o + 2], w[:, c, :], xm[:, c, c_r:c_r + 1], c == 0, c == DC - 1, wb + [rb], [B_ps[4]])
            pe_v = psum[4][:, 0:32].rearrange("p (c s e) -> p c s e", c=DC, s=2)
            pg.emit("act", lambda e: e.activation(out=zedge[:], in_=pe_v, func=AF.Identity), r=[B_ps[4]], w=[B_ze])
            tt(zedge2[:], zedge[:, :, 0, :], zedge[:, :, 1, :], MUL, r=[B_ze], w=[B_ze2])
            for cc in range(DC):
                bc_, bx_, bb_ = (0, 1, 2) if cc % 2 == 0 else (3, 5, 6)
                proj("c", cc, b, bc_)
                proj("x", cc, b, bx_)
                proj("b", cc, b, bb_)
                t, tb = tmp()
                z, zb = tmp()
                act(t[:, 0:TB], psum[bc_][:], AF.Identity, r=[B_ps[bc_]], w=[tb])
                tt(z[:, 1:TB + 1], psum[bx_][:], t[:, 0:TB], MUL, r=[B_ps[bx_], tb], w=[zb])
                pg.emit("dve", lambda e, z=z, cc=cc: e.tensor_copy(out=z[:, 0:1], in_=zedge2[:, cc, 0:1]), r=[B_ze2, zb], w=[zb])
                pg.emit("dve", lambda e, z=z, cc=cc: e.tensor_copy(out=z[:, TB + 1:TB + 2], in_=zedge2[:, cc, 1:2]), r=[B_ze2, zb], w=[zb])
                a, ab = tmp()
                ts(a[:, 0:TB], z[:, 1:TB + 1], convw[:, 1, cc:cc + 1], MUL, r=[zb, B_const], w=[ab])
                stt(a[:, 0:TB], z[:, 0:TB], convw[:, 0, cc:cc + 1], a[:, 0:TB], MUL, ADD, r=[zb, B_const, ab], w=[ab])
                stt(a[:, 0:TB], z[:, 2:TB + 2], convw[:, 2, cc:cc + 1], a[:, 0:TB], MUL, ADD, r=[zb, B_const, ab], w=[ab])
                tt(cgT[cc], psum[bb_][:], a[:, 0:TB], MUL, r=[B_ps[bb_], ab], w=[B_cg[cc]])
            for co in range(DC):
                w, wb = load_ws(wpb_d[co, :, :])
                by_, bg_ = (0, 1) if co % 2 == 0 else (2, 3)
                for c in range(DC):
                    mm(psum[by_][:], w[:, c, :], cgT[c], c == 0, c == DC - 1, wb + [B_cg[c]], [B_ps[by_]])
                proj("gb", co, b, bg_)
                t, tb = tmp()
                act(t[:, 0:TB], psum[bg_][:], AF.Sigmoid, r=[B_ps[bg_]], w=[tb])
                tt(yT[:, co, :], psum[by_][:], t[:, 0:TB], MUL, r=[B_ps[by_], tb], w=[B_y[co]])

        def merge_out(b):
            set_ws(extraB)
            for co in range(DC):
                w, wb = load_ws(wpa_d[co, :, :])
                by_, bg_ = (0, 1) if co % 2 == 0 else (2, 3)
                for h in range(8):
                    mm(psum[by_][:], w[:, h, :], onT[h], h == 0, h == 7, wb + [B_on[h]], [B_ps[by_]])
                proj("ga", co, b, bg_)
                t, tb = tmp()
                act(t[:, 0:TB], psum[bg_][:], AF.Sigmoid, r=[B_ps[bg_]], w=[tb])
                t2, t2b = tmp()
                tt(t2[:, 0:TB], psum[by_][:], t[:, 0:TB], MUL, r=[B_ps[by_], tb], w=[t2b])
                tt(mrg[co], t2[:, 0:TB], yT[:, co, :], ADD, r=[t2b, B_y[co]], w=[B_mrg[co]], eng="dve")
            pend = None
            for co in range(DC):
                w, wb = load_ws(wo_d[co, :, :])
                bank = 4 + (co % 2)
                for c in range(DC):
                    mm(psum[bank][:], w[:, c, :], mrg[c], c == 0, c == DC - 1, wb + [B_mrg[c]], [B_ps[bank]])
                if pend is not None:
                    pend()
                pend = y_evac(bank, co, TB)
            pend()
            postnorm_update(lat[b], 5)

        for b in range(NB):
            attention_block(b)
            merge_out(b)

    set_ws(())
    if stage >= 3:
        ffn_layer([[lat[0], lat[1]], [lat[2], lat[3]]], 1, 6, 8)

    for b in range(NB):
        dma("sp", out_d.ap().rearrange("(c p) t -> p c t", p=P)[:, :, b * TB:(b + 1) * TB], hT[:, :, b * TB:(b + 1) * TB],
            r=[B_h[c][b] for c in range(DC)], w=[B_out[b]], sem=S_out)
    pg.final.append(S_out)

    pg.finalize(esems)
    with nc.Block() as block:
        @block.tensor
        def _(e):
            pg.replay("pe", e)

        @block.scalar
        def _(e):
            pg.replay("act", e)

        @block.vector
        def _(e):
            pg.replay("dve", e)

        @block.gpsimd
        def _(e):
            pg.replay("pool", e)

        @block.sync
        def _(e):
            pg.replay("sp", e)
    es.close()
    return nc


def _panels(W, pw):
    K, N = W.shape
    kc, npn = K // P, N // pw
    return np.ascontiguousarray(W.reshape(kc, P, npn, pw).transpose(2, 1, 0, 3).reshape(npn, P, kc * pw))


def _chunked(vec):
    return np.ascontiguousarray(vec.reshape(-1, P).T)


def _rope_tables(t0):
    pos = np.arange(t0, t0 + T)
    row = (pos // 64).astype(np.float32)
    col = (pos % 64).astype(np.float32)
    inv_freq = (np.float32(10000.0) ** (np.float32(-2.0) * np.arange(16, dtype=np.float32) / np.float32(32))).astype(np.float32)
    ang = np.zeros((T, 64), np.float32)
    sgn = np.zeros((64,), np.float32)
    for d in range(64):
        axis, half, fr = d // 32, (d % 32) // 16, d % 16
        ang[:, d] = (row if axis == 0 else col) * inv_freq[fr]
        sgn[d] = -1.0 if half == 0 else 1.0
    cos = np.cos(ang).astype(np.float32)
    sin = (np.sin(ang).astype(np.float32)) * sgn[None, :]
    cT = np.ascontiguousarray(np.concatenate([cos.T, cos.T], 0))
    sT = np.ascontiguousarray(np.concatenate([sin.T, sin.T], 0))
    return cT, sT


def _perm_cols():
    idx = np.arange(1024)
    d = idx % 64
    half = (d % 32) // 16
    return np.where(half == 0, idx + 16, idx - 16)


_CACHE = {}


def kernel(x, c, ctx, c_ctx, w_mod, b_mod, ffn1_pre_g, ffn1_post_g, ffn1_w_gate, ffn1_w_up, ffn1_w_down,
           mix_pre_g, mix_post_g, w_in, lam_q1, lam_k1, lam_q2, lam_k2, attn_subln_g, conv_w,
           w_attn_proj, w_conv_proj, w_out, ffn2_pre_g, ffn2_post_g, ffn2_w_gate, ffn2_w_up, ffn2_w_down, _stage=3):
    f = lambda a: np.asarray(a, dtype=np.float32)
    x, c, ctx, c_ctx = f(x), f(c), f(ctx), f(c_ctx)
    w_in0 = f(w_in)[0]
    perm = _perm_cols()
    secs = {"q": w_in0[:, 0:1024], "k": w_in0[:, 1024:2048], "b": w_in0[:, 3072:4096], "c": w_in0[:, 4096:5120],
            "x": w_in0[:, 5120:6144], "ga": w_in0[:, 6144:7168], "gb": w_in0[:, 7168:8192]}
    secs["qp"] = secs["q"][:, perm]
    secs["kp"] = secs["k"][:, perm]
    wins = np.concatenate([_panels(secs[k], 128) for k in ("q", "qp", "k", "kp", "b", "c", "x", "ga", "gb")], 0)
    winv = _panels(w_in0[:, 2048:3072], 256)
    shared = {
        "wmod": _panels(f(w_mod)[0], 128),
        "bmod": np.ascontiguousarray(f(b_mod)[0].reshape(72, P).T),
        "gains": np.ascontiguousarray(np.stack([_chunked(f(g)[0]) for g in
                                                (ffn1_pre_g, ffn1_post_g, mix_pre_g, mix_post_g, ffn2_pre_g, ffn2_post_g)], 1)),
        "subg": np.ascontiguousarray(f(attn_subln_g)[0].reshape(P, 1)),
        "convw": np.ascontiguousarray(np.stack([_chunked(f(conv_w)[0][j]) for j in range(3)], 1)),
        "lamv": np.ascontiguousarray(np.broadcast_to(np.stack([f(lam_q1)[0], f(lam_k1)[0], f(lam_q2)[0], f(lam_k2)[0]], 0)[None], (P, 4, 64))),
        "w1g": _panels(f(ffn1_w_gate)[0], 128), "w1u": _panels(f(ffn1_w_up)[0], 128), "w1d": _panels(f(ffn1_w_down)[0], 128),
        "w2g": _panels(f(ffn2_w_gate)[0], 128), "w2u": _panels(f(ffn2_w_up)[0], 128), "w2d": _panels(f(ffn2_w_down)[0], 128),
        "wins": wins, "winv": winv,
        "wpa": _panels(f(w_attn_proj)[0], 128), "wpb": _panels(f(w_conv_proj)[0], 128), "wo": _panels(f(w_out)[0], 128),
    }
    in_maps = []
    for core in range(8):
        b, r = core // 4, core % 4
        t0 = r * T
        cT, sT = _rope_tables(t0)
        sel = np.zeros((P, 8), np.float32)
        if r > 0:
            sel[:, r - 1] = 1.0
        if r < 3:
            sel[:, 4 + r + 1] = 1.0
        m = dict(shared)
        m["xT"] = np.ascontiguousarray(x[b, t0:t0 + T, :].T)
        m["ctxT"] = np.ascontiguousarray(ctx[b, r * NCTX:(r + 1) * NCTX, :].T)
        m["cvec"] = np.ascontiguousarray(np.stack([_chunked(c[b]), _chunked(c_ctx)], -1))
        m["ropec"], m["ropes"], m["sel"] = cT, sT, sel
        in_maps.append(m)
    if _stage not in _CACHE:
        _CACHE[_stage] = build(_stage)
    nc = _CACHE[_stage]
    res = run_bass_kernel_spmd(nc, in_maps, core_ids=list(range(8)))
    out = np.empty((2, 4 * T, 1024), np.float32)
    for core in range(8):
        b, r = core // 4, core % 4
        out[b, r * T:(r + 1) * T, :] = np.asarray(res.results[core]["outT"]).T
    return out
```

```python
import math
from contextlib import ExitStack
import numpy as np
import concourse.bass as bass
import concourse.mybir as mybir
from concourse.bass_utils import run_bass_kernel_spmd

F32 = mybir.dt.float32
BF16 = mybir.dt.bfloat16
AF = mybir.ActivationFunctionType
ALU = mybir.AluOpType

P = 128
T = 2048
TB = 512
NB = 4
DC = 8
FC = 22
NCTX = 64
KW = T + NCTX
NKT = 17
EPS = 1e-6
LAM_INIT = 0.8 - 0.6 * math.exp(-0.3 * 0)
GROUPS = [[0, 1, 2, 3], [4, 5, 6, 7]]
SEC = {"q": 0, "qp": 1, "k": 2, "kp": 3, "b": 4, "c": 5, "x": 6, "ga": 7, "gb": 8}


class Buf:
    __slots__ = ("name", "excl", "w", "rd")

    def __init__(self, name, excl=False):
        self.name = name
        self.excl = excl
        self.w = None
        self.rd = []


class Sem:
    def __init__(self, h):
        self.h = h
        self.count = 0


class Op:
    __slots__ = ("eng", "fn", "deps", "signal", "val", "sem", "amt")

    def __init__(self, eng, fn):
        self.eng = eng
        self.fn = fn
        self.deps = ()
        self.signal = False
        self.val = 0
        self.sem = None
        self.amt = 1


class Prog:
    ENGS = ("pe", "act", "dve", "pool", "sp")

    def __init__(self):
        self.ops = {e: [] for e in self.ENGS}
        self.final = []

    def emit(self, eng, fn, r=(), w=(), sem=None, amt=16):
        op = Op(eng, fn)
        deps = set()
        for b in r:
            if b.w is not None:
                deps.add(b.w)
            if b.excl:
                for o in b.rd:
                    if o.eng != eng:
                        deps.add(o)
        for b in w:
            if b.w is not None:
                deps.add(b.w)
            deps.update(b.rd)
        if eng == "pe":
            deps = {d for d in deps if d.eng != "pe"}
        for d in deps:
            d.signal = True
        op.deps = deps
        for b in r:
            b.rd.append(op)
        for b in w:
            b.w = op
            b.rd = []
        if sem is not None:
            sem.count += amt
            op.sem, op.val, op.amt, op.signal = sem, sem.count, amt, True
        self.ops[eng].append(op)
        return op

    def finalize(self, esems):
        for e in self.ENGS:
            s = esems[e]
            for op in self.ops[e]:
                if op.sem is None and op.signal:
                    s.count += 1
                    op.sem, op.val, op.amt = s, s.count, 1

    def replay(self, eng, e):
        waited = {}
        for op in self.ops[eng]:
            need = {}
            for d in op.deps:
                if need.get(d.sem, 0) < d.val:
                    need[d.sem] = d.val
            for s, v in need.items():
                if waited.get(s, 0) < v:
                    e.wait_ge(s.h, v)
                    waited[s] = v
            ins = op.fn(e)
            if op.signal:
                ins.then_inc(op.sem.h, op.amt)
        if eng == "sp":
            for s in self.final:
                e.wait_ge(s.h, s.count)


def build(stage=3):
    nc = bass.Bass("TRN2", target_bir_lowering=False)
    es = ExitStack()

    def din(name, shape, dt=F32):
        return nc.dram_tensor(name, list(shape), dt, kind="ExternalInput")

    xT_d = din("xT", [1024, T])
    ctxT_d = din("ctxT", [1024, NCTX])
    cvec_d = din("cvec", [P, DC, 2])
    wmod_d = din("wmod", [72, P, DC * 128])
    bmod_d = din("bmod", [P, 72])
    gains_d = din("gains", [P, 6, DC])
    subg_d = din("subg", [P, 1])
    convw_d = din("convw", [P, 3, DC])
    lamv_d = din("lamv", [P, 4, 64])
    ropec_d = din("ropec", [P, T])
    ropes_d = din("ropes", [P, T])
    sel_d = din("sel", [P, 8])
    ffw_d = []
    for i in (1, 2):
        ffw_d.append((din(f"w{i}g", [FC, P, DC * 128]), din(f"w{i}u", [FC, P, DC * 128]),
                      din(f"w{i}d", [DC, P, FC * 128])))
    wins_d = din("wins", [72, P, DC * 128])
    winv_d = din("winv", [4, P, DC * 256])
    wpa_d = din("wpa", [DC, P, DC * 128])
    wpb_d = din("wpb", [DC, P, DC * 128])
    wo_d = din("wo", [DC, P, DC * 128])
    out_d = nc.dram_tensor("outT", [1024, T], F32, kind="ExternalOutput")

    kb_d = [nc.dram_tensor(f"kb{h}", [P, KW], BF16) for h in range(8)]
    kg_d = [nc.dram_tensor(f"kg{h}", [4 * P, KW], BF16) for h in range(8)]
    vb_d = [nc.dram_tensor(f"vb{h}", [P, NKT * 128], BF16) for h in range(8)]
    vg_d = [nc.dram_tensor(f"vg{h}", [4 * P, NKT * 128], BF16) for h in range(8)]
    hb_d = nc.dram_tensor("hb", [P, 16], BF16)
    hg_d = nc.dram_tensor("hg", [4 * P, 16], BF16)

    def sb(name, shape, dt):
        return es.enter_context(nc.sbuf_tensor(name, list(shape), dt))

    hT = sb("hT", [P, DC, T], F32)
    hc = sb("hc", [P, DC, NCTX], F32)
    xm = sb("xm", [P, DC, T + 2], BF16)
    xmc = sb("xmc", [P, DC, NCTX], BF16)
    arena = sb("arena", [P, 32 * 512], BF16)
    yT = sb("yT", [P, DC, TB], F32)
    NTMP = 4
    tmps = [sb(f"tmp{i}", [P, TB + 2], F32) for i in range(NTMP)]
    NWS = 4
    wsl = [sb(f"ws{i}", [P, DC, 128], BF16) for i in range(NWS)]
    wdl = [sb(f"wd{i}", [P, FC, 128], BF16) for i in range(2)]
    rstd_t = [sb(f"rstd{i}", [P, TB], F32) for i in range(1)]
    accs = [None, sb("acc1", [P, TB], F32)]
    ropeCs = [sb(f"ropeC{i}", [P, TB], F32) for i in range(2)]
    ropeSs = [sb(f"ropeS{i}", [P, TB], F32) for i in range(2)]
    cvec = sb("cvec_s", [P, DC, 2], F32)
    scv = sb("scv", [P, DC, 2], BF16)
    bmod = sb("bmod_s", [P, 72], F32)
    modv = sb("modv", [P, 72, 2], F32)
    gains = sb("gains_s", [P, 6, DC], F32)
    subg = sb("subg_s", [P, 1], F32)
    subg2 = sb("subg2", [P, 1], F32)
    convw = sb("convw_s", [P, 3, DC], F32)
    lamv = sb("lamv_s", [P, 4, 64], F32)
    lamt = sb("lamt", [P, 2, 64], F32)
    lams = sb("lams", [P, 4], F32)
    nlam = sb("nlam", [P, 1], F32)
    sel = sb("sel_s", [P, 8], F32)
    coef = sb("coef", [P, 2, 9, DC], F32)
    ones_b = sb("ones_b", [P, P], BF16)
    ones_f = sb("ones_f", [P, P], F32)
    hgs = sb("hgs", [P, 4, 16], BF16)
    hst = sb("hst", [P, 2, DC], BF16)
    eps_t = sb("eps_t", [P, 1], F32)
    hacc = sb("hacc", [P, 2, DC], F32)
    zedge = sb("zedge", [P, DC, 2, 2], F32)
    zedge2 = sb("zedge2", [P, DC, 2], F32)
    psum = [es.enter_context(nc.psum_tensor(f"ps{i}", [P, TB], F32)) for i in range(8)]

    def mksem(name):
        return Sem(es.enter_context(nc.semaphore(name)))

    esems = {e: mksem("e_" + e) for e in Prog.ENGS}
    pg = Prog()

    B_h = [[Buf(f"h{c}_{b}") for b in range(NB)] for c in range(DC)]
    B_hc = Buf("hc")
    B_xm = [Buf(f"xm{b}") for b in range(NB)]
    B_xmL, B_xmR = Buf("xmL"), Buf("xmR")
    B_xmc = Buf("xmc")
    B_pg = [Buf(f"pg{i}") for i in range(32)]
    B_y = [Buf(f"y{c}") for c in range(DC)]
    B_tmp = [Buf(f"tmp{i}") for i in range(NTMP)]
    B_ws = [Buf(f"ws{i}") for i in range(NWS)]
    B_wd = [Buf(f"wd{i}") for i in range(2)]
    B_ps = [Buf(f"ps{i}", excl=True) for i in range(8)]
    B_ropeCs = [Buf("ropeC0"), Buf("ropeC1")]
    B_ropeSs = [Buf("ropeS0"), Buf("ropeS1")]
    B_ze, B_ze2, B_hacc, B_hst = Buf("ze"), Buf("ze2"), Buf("hacc"), Buf("hst")
    B_rstd = [Buf(f"rstd{i}") for i in range(2)]
    B_acc = [Buf(f"acc{i}") for i in range(2)]
    B_const = Buf("const")
    B_modv = Buf("modv")
    B_coef = Buf("coef")
    B_misc = Buf("misc")
    B_kb = [Buf(f"kb{h}") for h in range(8)]
    B_kg = [Buf(f"kg{h}") for h in range(8)]
    B_vb = [Buf(f"vb{h}") for h in range(8)]
    B_vg = [Buf(f"vg{h}") for h in range(8)]
    B_hb, B_hg, B_hgs = Buf("hb"), Buf("hg"), Buf("hgs")
    B_out = [Buf(f"out{b}") for b in range(NB)]

    S_ws = [mksem(f"s_ws{i}") for i in range(NWS)]
    S_wd = [mksem(f"s_wd{i}") for i in range(2)]
    S_x = [mksem(f"s_x{i}") for i in range(NB)]
    S_const = mksem("s_const")
    S_hc = mksem("s_hc")
    S_ropeCs = [mksem("s_ropeC0"), mksem("s_ropeC1")]
    S_ropeSs = [mksem("s_ropeS0"), mksem("s_ropeS1")]
    S_kc = [mksem(f"s_kc{i}") for i in range(2)]
    S_vc = [mksem(f"s_vc{i}") for i in range(2)]
    S_kb = [mksem(f"s_kb{h}") for h in range(8)]
    S_vb = [mksem(f"s_vb{h}") for h in range(8)]
    S_kg = [mksem(f"s_kg{h}") for h in range(8)]
    S_vg = [mksem(f"s_vg{h}") for h in range(8)]
    S_hb, S_hg, S_hgs = mksem("s_hb"), mksem("s_hg"), mksem("s_hgs")
    S_out = mksem("s_out")
    S_wv = [mksem(f"s_wv{i}") for i in range(2)]

    rot = {"tmp": 0, "ws": 0, "wd": 0, "rstd": 0, "ps": 0}

    def tmp():
        i = rot["tmp"]
        rot["tmp"] = (i + 1) % NTMP
        return tmps[i], B_tmp[i]

    def page(i, n=1):
        return arena[:, i * 512:(i + n) * 512]

    def mm(out, lhsT, rhs, start, stop, r, w):
        pg.emit("pe", lambda e, o=out, l=lhsT, rr=rhs, s=start, t=stop: e.matmul(o, lhsT=l, rhs=rr, start=s, stop=t),
                r=r, w=w)

    def act(out, in_, func, r, w, scale=None, bias=None):
        kw = {}
        if scale is not None:
            kw["scale"] = scale
        if bias is not None:
            kw["bias"] = bias
        pg.emit("act", lambda e, o=out, i=in_, f=func, k=kw: e.activation(out=o, in_=i, func=f, **k), r=r, w=w)

    def tt(out, in0, in1, op, r, w, eng="dve"):
        pg.emit(eng, lambda e, o=out, a=in0, b=in1, p=op: e.tensor_tensor(out=o, in0=a, in1=b, op=p), r=r, w=w)

    def ts(out, in0, s1, op0, r, w, s2=None, op1=None, eng="dve"):
        if op1 is None:
            pg.emit(eng, lambda e, o=out, a=in0, s=s1, p=op0: e.tensor_scalar(out=o, in0=a, scalar1=s, scalar2=None, op0=p),
                    r=r, w=w)
        else:
            pg.emit(eng, lambda e, o=out, a=in0, s=s1, p=op0, ss=s2, pp=op1:
                    e.tensor_scalar(out=o, in0=a, scalar1=s, scalar2=ss, op0=p, op1=pp), r=r, w=w)

    def stt(out, in0, scalar, in1, op0, op1, r, w):
        pg.emit("dve", lambda e, o=out, a=in0, s=scalar, b=in1, p0=op0, p1=op1:
                e.scalar_tensor_tensor(out=o, in0=a, scalar=s, in1=b, op0=p0, op1=p1), r=r, w=w)

    def dma(q, out, in_, r, w, sem):
        return pg.emit(q, lambda e, o=out, i=in_: e.dma_start(out=o, in_=i), r=r, w=w, sem=sem, amt=16)

    ws_slots = [(wsl[i][:].rearrange("p c n -> p (c n)"), wsl[i], [B_ws[i]], S_ws[i]) for i in range(NWS)]
    ws_state = {"slots": ws_slots, "i": 0}

    def load_ws(src_ap):
        sl = ws_state["slots"]
        i = ws_state["i"] % len(sl)
        ws_state["i"] = i + 1
        flat, w3, bufs, sem = sl[i]
        dma("pool", flat, src_ap, r=[], w=bufs, sem=sem)
        return w3, bufs

    def set_ws(extra=()):
        ws_state["slots"] = ws_slots + list(extra)
        ws_state["i"] = 0

    for dst, src in ((cvec, cvec_d), (bmod, bmod_d), (gains, gains_d), (subg, subg_d), (convw, convw_d),
                     (lamv, lamv_d), (sel, sel_d)):
        dma("sp", dst[:], src.ap(), r=[], w=[B_const], sem=S_const)
    for b in range(NB):
        dma("sp", hT[:, :, b * TB:(b + 1) * TB],
            xT_d.ap().rearrange("(c p) t -> p c t", p=P)[:, :, b * TB:(b + 1) * TB],
            r=[], w=[B_h[c][b] for c in range(DC)], sem=S_x[b])
    dma("sp", hc[:], ctxT_d.ap().rearrange("(c p) t -> p c t", p=P), r=[], w=[B_hc], sem=S_hc)
    pg.emit("dve", lambda e: e.memset(ones_b[:], 1.0), r=[], w=[B_misc])
    pg.emit("dve", lambda e: e.memset(ones_f[:], 1.0), r=[B_misc], w=[B_misc])
    pg.emit("dve", lambda e: e.memset(eps_t[:], EPS), r=[B_misc], w=[B_misc])

    def rstd_tile():
        i = rot["rstd"]
        rot["rstd"] = 0
        return rstd_t[i], B_rstd[i]

    AX = mybir.AxisListType.X
    MUL, ADD, SUB = ALU.mult, ALU.add, ALU.subtract

    act(scv[:], cvec[:], AF.Silu, r=[B_const], w=[B_misc])
    B_modvs = [Buf(f"modv{i}") for i in range(3)]
    B_coefs = [Buf(f"coef{i}") for i in range(3)]

    def mod_part_steps(s_, bank):
        j0 = 24 * s_
        steps = []

        def panel(j):
            bk = bank + (j % 2)
            wt, wb = load_ws(wmod_d[j, :, :])
            for c in range(DC):
                mm(psum[bk][:, 0:2], wt[:, c, :], scv[:, c, :], c == 0, c == DC - 1, wb + [B_misc], [B_ps[bk]])
            ts(modv[:, j, :], psum[bk][:, 0:2], bmod[:, j:j + 1], ADD, r=[B_ps[bk], B_const], w=[B_modvs[s_]])

        def final():
            half = 0.5 if s_ != 1 else 1.0
            for v in range(2):
                stt(coef[:, v, 3 * s_, :], modv[:, (3 * s_ + 1) * 8:(3 * s_ + 2) * 8, v], 1.0, gains[:, 2 * s_, :], ADD, MUL,
                    r=[B_modvs[s_], B_const], w=[B_coefs[s_]])
                ts(coef[:, v, 3 * s_ + 1, :], modv[:, (3 * s_) * 8:(3 * s_ + 1) * 8, v], 1.0, MUL, r=[B_modvs[s_]], w=[B_coefs[s_]])
                stt(coef[:, v, 3 * s_ + 2, :], modv[:, (3 * s_ + 2) * 8:(3 * s_ + 3) * 8, v], half, gains[:, 2 * s_ + 1, :], MUL, MUL,
                    r=[B_modvs[s_], B_const], w=[B_coefs[s_]])

        for j in range(j0, j0 + 24):
            steps.append(lambda j=j: panel(j))
        steps.append(final)
        return steps

    def mod_part(s_, bank):
        for st_ in mod_part_steps(s_, bank):
            st_()

    pending_mod = []

    mod_part(0, 0)
    tt(lamt[:, 0, :], lamv[:, 0, :], lamv[:, 1, :], MUL, r=[B_const], w=[B_misc])
    tt(lamt[:, 1, :], lamv[:, 2, :], lamv[:, 3, :], MUL, r=[B_const, B_misc], w=[B_misc])
    pg.emit("dve", lambda e: e.tensor_reduce(out=lams[:, 0:2], in_=lamt[:], axis=AX, op=ADD), r=[B_misc], w=[B_misc])
    act(lams[:, 2:4], lams[:, 0:2], AF.Exp, r=[B_misc], w=[B_misc])
    stt(nlam[:], lams[:, 3:4], -LAM_INIT, lams[:, 2:3], ADD, SUB, r=[B_misc], w=[B_misc])
    ts(subg2[:], subg[:], 1.0 - LAM_INIT, MUL, r=[B_const, B_misc], w=[B_misc])

    class Blk:
        def __init__(self, idx):
            self.idx = idx
            if idx >= 0:
                self.n, self.v = TB, 0
                t0 = idx * TB
                self.h = lambda c: hT[:, c, t0:t0 + TB]
                self.hb = lambda c: B_h[c][idx]
            else:
                self.n, self.v = NCTX, 1
                self.h = lambda c: hc[:, c, :]
                self.hb = lambda c: B_hc

    def stats_rstd(bank, n, nfeat):
        rs, rsb = rstd_tile()
        act(rs[:, 0:n], psum[bank][:, 0:n], AF.Ln, r=[B_ps[bank]], w=[rsb], scale=1.0 / nfeat, bias=eps_t[:, 0:1])
        act(rs[:, 0:n], rs[:, 0:n], AF.Exp, r=[rsb], w=[rsb], scale=-0.5)
        return rs, rsb

    xq_bufs = {}

    def prenorm(blk, kA, xh, xb):
        n, v = blk.n, blk.v
        for c in range(DC):
            qb = xq_bufs.setdefault((id(xb), c), Buf("xq"))
            if c == 0:
                act(xh(c), blk.h(c), AF.Square, r=[blk.hb(c)], w=[xb, qb])
            else:
                act(xh(c), blk.h(c), AF.Square, r=[blk.hb(c), xb], w=[qb])
            mm(psum[6][:, 0:n], ones_b[:], xh(c), c == 0, c == DC - 1, [B_misc, qb], [B_ps[6]])
        rs, rsb = stats_rstd(6, n, 1024.0)
        for c in range(DC):
            qb = xq_bufs[(id(xb), c)]
            t, tb = tmp()
            stt(t[:, 0:n], blk.h(c), coef[:, v, kA, c:c + 1], rs[:, 0:n], MUL, MUL, r=[blk.hb(c), B_coefs[kA // 3], rsb], w=[tb])
            act(xh(c), t[:, 0:n], AF.Identity, r=[tb, B_coefs[kA // 3]], w=[xb, qb], bias=coef[:, v, kA + 1, c:c + 1])

    def postnorm_update(blk, kB):
        n, v = blk.n, blk.v
        rs, rsb = stats_rstd(7, n, 1024.0)
        for c in range(DC):
            t, tb = tmp()
            stt(t[:, 0:n], yT[:, c, 0:n], coef[:, v, kB, c:c + 1], rs[:, 0:n], MUL, MUL, r=[B_y[c], B_coefs[kB // 3], rsb], w=[tb])
            tt(blk.h(c), blk.h(c), t[:, 0:n], ADD, r=[blk.hb(c), tb], w=[blk.hb(c)], eng="dve")

    def y_evac(bank, c, n):
        pg.emit("dve", lambda e, o=yT[:, c, 0:n], i=psum[bank][:, 0:n]: e.tensor_copy(out=o, in_=i), r=[B_ps[bank]], w=[B_y[c]])
        t, tb = tmp()
        act(t[:, 0:n], yT[:, c, 0:n], AF.Square, r=[B_y[c]], w=[tb])
        return lambda: mm(psum[7][:, 0:n], ones_f[:], t[:, 0:n], c == 0, c == DC - 1, [B_misc, tb], [B_ps[7]])

    B_pgx = [Buf(f"pgx{i}") for i in range(16)]
    bank_ctr = [0]

    def act_page(f, si, n, is_ctx):
        if is_ctx:
            idx, c0 = 44 + f // 8, (f % 8) * NCTX
        else:
            idx, c0 = 2 * f + si, 0
        if idx < 32:
            return arena[:, idx * 512 + c0:idx * 512 + c0 + n], [B_pg[idx]]
        i = idx - 32
        col = 1025 + (i // 8) * 512 + c0
        return xm[:, i % 8, col:col + n], [B_pgx[i], B_xm[2 + i // 8]]

    def xhat_of(blk, si):
        if blk.idx >= 0:
            return (lambda c: xm[:, c, 1 + si * TB:1 + (si + 1) * TB]), B_xm[si]
        return (lambda c: xmc[:, c, :]), B_xmc

    def ffn_super(blocks, wg_d, wu_d, wd_d, kB, between=None):
        for f in range(FC):
            wg, wgb = load_ws(wg_d[f, :, :])
            wu, wub = load_ws(wu_d[f, :, :])
            for si, blk in enumerate(blocks):
                n = blk.n
                xh, xb = xhat_of(blk, si)
                k = bank_ctr[0] % 2
                bank_ctr[0] += 1
                gb_, ub_ = k, 2 + k
                for c in range(DC):
                    mm(psum[gb_][:, 0:n], wg[:, c, :], xh(c), c == 0, c == DC - 1, wgb + [xb], [B_ps[gb_]])
                for c in range(DC):
                    mm(psum[ub_][:, 0:n], wu[:, c, :], xh(c), c == 0, c == DC - 1, wub + [xb], [B_ps[ub_]])
                t, tb = tmp()
                act(t[:, 0:n], psum[gb_][:, 0:n], AF.Silu, r=[B_ps[gb_]], w=[tb])
                ap_, bufs_ = act_page(f, si, n, blk.idx < 0)
                tt(ap_, psum[ub_][:, 0:n], t[:, 0:n], MUL, r=[B_ps[ub_], tb], w=bufs_)
        if between is not None:
            between()
        for si, blk in enumerate(blocks):
            n = blk.n
            pend = None
            for co in range(DC):
                i = rot["wd"]
                rot["wd"] = (i + 1) % 2
                dma("pool", wdl[i][:].rearrange("p f n -> p (f n)"), wd_d[co, :, :], r=[], w=[B_wd[i]], sem=S_wd[i])
                bank = 4 + (co % 2)
                for f in range(FC):
                    ap_, bufs_ = act_page(f, si, n, blk.idx < 0)
                    mm(psum[bank][:, 0:n], wdl[i][:, f, :], ap_, f == 0, f == FC - 1, [B_wd[i]] + bufs_, [B_ps[bank]])
                if pend is not None:
                    pend()
                pend = y_evac(bank, co, n)
                for _ in range(2):
                    if pending_mod:
                        pending_mod.pop(0)()
            pend()
            postnorm_update(blk, kB)

    def ffn_layer(supers, widx, kA, kB, after_first=None):
        wg_d, wu_d, wd_d = ffw_d[widx]

        def prenorm_super(blocks):
            for si, blk in enumerate(blocks):
                xh, xb = xhat_of(blk, si)
                prenorm(blk, kA, xh, xb)

        prenorm_super(supers[0])
        for i, blocks in enumerate(supers):
            def between(i=i):
                if i == 0 and after_first is not None:
                    after_first()
                if i + 1 < len(supers):
                    prenorm_super(supers[i + 1])
            ffn_super(blocks, wg_d, wu_d, wd_d, kB, between)

    lat = [Blk(b) for b in range(NB)]
    ctxb = Blk(-1)

    pending_mod.extend(mod_part_steps(1, 2) + mod_part_steps(2, 2))
    ffn_layer([[lat[0], lat[1], ctxb], [lat[2], lat[3]]], 0, 0, 2)
    while pending_mod:
        pending_mod.pop(0)()

    if stage >= 2:
        for b in range(NB):
            prenorm(lat[b], 3, (lambda c, b=b: xm[:, c, 1 + b * TB:1 + (b + 1) * TB]), B_xm[b])
        prenorm(ctxb, 3, (lambda c: xmc[:, c, :]), B_xmc)

        pg.emit("dve", lambda e: e.tensor_copy(out=hst[:, 0, :], in_=xm[:, :, 1]), r=[B_xm[0]], w=[B_hst])
        pg.emit("dve", lambda e: e.tensor_copy(out=hst[:, 1, :], in_=xm[:, :, T]), r=[B_xm[NB - 1], B_hst], w=[B_hst])
        dma("sp", hb_d.ap(), hst[:].rearrange("p s c -> p (s c)"), r=[B_hst], w=[B_hb], sem=S_hb)
        pg.emit("pool", lambda e: e.collective_compute("AllGather", ALU.bypass, replica_groups=GROUPS,
                                                       ins=[hb_d.ap().opt()], outs=[hg_d.ap().opt()]),
                r=[B_hb], w=[B_hg], sem=S_hg, amt=1)
        dma("sp", hgs[:], hg_d.ap().rearrange("(r p) n -> p r n", p=P), r=[B_hg], w=[B_hgs], sem=S_hgs)
        hgv = hgs[:].rearrange("p r (s c) -> p r s c", s=2)
        for side in range(2):
            src_s = 1 - side
            ts(hacc[:, side, :], hgv[:, 0, src_s, :], sel[:, side * 4:side * 4 + 1], MUL, r=[B_hgs, B_const], w=[B_hacc])
            for r_ in range(1, 4):
                stt(hacc[:, side, :], hgv[:, r_, src_s, :], sel[:, side * 4 + r_:side * 4 + r_ + 1], hacc[:, side, :], MUL, ADD,
                    r=[B_hgs, B_const, B_hacc], w=[B_hacc])
        pg.emit("dve", lambda e: e.tensor_copy(out=xm[:, :, 0], in_=hacc[:, 0, :]), r=[B_hacc], w=[B_xmL])
        pg.emit("dve", lambda e: e.tensor_copy(out=xm[:, :, T + 1], in_=hacc[:, 1, :]), r=[B_hacc], w=[B_xmR])

        rope_ctr = [0]
        ropeC_v = [ropeCs[0][:], ropeCs[1][:], page(23, 2).bitcast(F32), page(27, 2).bitcast(F32)]
        ropeS_v = [ropeSs[0][:], ropeSs[1][:], page(25, 2).bitcast(F32), page(29, 2).bitcast(F32)]
        B_ropeC_v = [[B_ropeCs[0]], [B_ropeCs[1]], B_pg[23:25], B_pg[27:29]]
        B_ropeS_v = [[B_ropeSs[0]], [B_ropeSs[1]], B_pg[25:27], B_pg[29:31]]
        S_rope_x = [mksem(f"s_ropex{i}") for i in range(4)]

        def load_rope(b, ri=None):
            if ri is None:
                ri = rope_ctr[0] % 2
                rope_ctr[0] += 1
            if ri < 2:
                sc_, ss_ = S_ropeCs[ri], S_ropeSs[ri]
            else:
                sc_, ss_ = S_rope_x[(ri - 2) * 2], S_rope_x[(ri - 2) * 2 + 1]
            dma("sp", ropeC_v[ri], ropec_d[:, b * TB:(b + 1) * TB], r=[], w=B_ropeC_v[ri], sem=sc_)
            dma("sp", ropeS_v[ri], ropes_d[:, b * TB:(b + 1) * TB], r=[], w=B_ropeS_v[ri], sem=ss_)
            return ri

        def rope_evac(bk_a, bk_p, out_ap, out_bufs, ri):
            t1, t1b = tmp()
            t2, t2b = tmp()
            tt(t1[:, 0:TB], psum[bk_a][:], ropeC_v[ri], MUL, r=[B_ps[bk_a]] + B_ropeC_v[ri], w=[t1b])
            tt(t2[:, 0:TB], psum[bk_p][:], ropeS_v[ri], MUL, r=[B_ps[bk_p]] + B_ropeS_v[ri], w=[t2b])
            tt(out_ap, t1[:, 0:TB], t2[:, 0:TB], ADD, r=[t1b, t2b], w=out_bufs, eng="dve")

        wvs = page(0, 4).rearrange("p (c n) -> p c n", c=DC)
        B_wvs = B_pg[0:4]
        vst = arena[:, 4 * 512:4 * 512 + 2 * NKT * 128].rearrange("p (h j e) -> p h j e", h=2, j=NKT)
        B_vst = B_pg[4:13]
        kst = [page(13, 5), page(18, 5)]
        B_kst = [B_pg[13:18], B_pg[18:23]]

        pending_cc = []

        def flush_cc():
            for fn_ in pending_cc:
                fn_()
            del pending_cc[:]

        def v_pair(hp):
            dma("pool", wvs.rearrange("p c n -> p (c n)"), winv_d[hp, :, :], r=[], w=B_wvs, sem=S_wv[0])
            for j in range(NKT):
                bank = j % 2
                nk = 128 if j < 16 else NCTX
                for c in range(DC):
                    lhsT = xm[:, c, 1 + j * 128:1 + (j + 1) * 128] if j < 16 else xmc[:, c, :]
                    rb = B_xm[j // 4] if j < 16 else B_xmc
                    mm(psum[bank][0:nk, 0:256], lhsT, wvs[:, c, :], c == 0, c == DC - 1, [rb] + B_wvs, [B_ps[bank]])
                act(vst[0:nk, :, j, :], psum[bank][0:nk, 0:256].rearrange("p (h e) -> p h e", h=2), AF.Identity,
                    r=[B_ps[bank]], w=B_vst)
            for hh in range(2):
                h = hp * 2 + hh
                dma("sp", vb_d[h].ap(), vst[:, hh, :, :].rearrange("p j e -> p (j e)"), r=B_vst, w=[B_vb[h]], sem=S_vb[h])
                pending_cc.append(lambda h=h: pg.emit("pool", lambda e, h=h: e.collective_compute(
                    "AllGather", ALU.bypass, replica_groups=GROUPS, ins=[vb_d[h].ap().opt()], outs=[vg_d[h].ap().opt()], dma_qos="P2"),
                    r=[B_vb[h]], w=[B_vg[h]], sem=S_vg[h], amt=1))

        def k_head(h):
            si = h % 2
            ks, ksb = kst[si], B_kst[si]
            wk, wkb = load_ws(wins_d[SEC["k"] * 8 + h, :, :])
            wp, wpb_ = load_ws(wins_d[SEC["kp"] * 8 + h, :, :])
            for b in range(NB):
                ri = b
                rhs_b = B_xm[b]
                ka, kp_ = (2, 3) if b % 2 == 0 else (4, 5)
                for c in range(DC):
                    mm(psum[ka][:], wk[:, c, :], xm[:, c, 1 + b * TB:1 + (b + 1) * TB], c == 0, c == DC - 1, wkb + [rhs_b], [B_ps[ka]])
                for c in range(DC):
                    mm(psum[kp_][:], wp[:, c, :], xm[:, c, 1 + b * TB:1 + (b + 1) * TB], c == 0, c == DC - 1, wpb_ + [rhs_b], [B_ps[kp_]])
                kb2, kb3 = (2, 3) if b % 2 == 0 else (4, 5)
                rope_evac(kb2, kb3, ks[:, b * TB:(b + 1) * TB], ksb, ri)
            for c in range(DC):
                mm(psum[2][:, 0:NCTX], wk[:, c, :], xmc[:, c, :], c == 0, c == DC - 1, wkb + [B_xmc], [B_ps[2]])
            act(ks[:, T:T + NCTX], psum[2][:, 0:NCTX], AF.Identity, r=[B_ps[2]], w=ksb)
            dma("sp", kb_d[h].ap(), ks[:, 0:KW], r=ksb, w=[B_kb[h]], sem=S_kb[h])
            flush_cc()
            pending_cc.append(lambda h=h: pg.emit("pool", lambda e, h=h: e.collective_compute(
                "AllGather", ALU.bypass, replica_groups=GROUPS, ins=[kb_d[h].ap().opt()], outs=[kg_d[h].ap().opt()], dma_qos="P2"),
                r=[B_kb[h]], w=[B_kg[h]], sem=S_kg[h], amt=1))

        for b in range(NB):
            load_rope(b, b)
        for hp in range(4):
            v_pair(hp)
            k_head(2 * hp)
            k_head(2 * hp + 1)
        flush_cc()

        qT = [page(h) for h in range(8)]
        B_q = B_pg[0:8]
        onT = [page(8 + h) for h in range(8)]
        B_on = B_pg[8:16]
        pT = [page(16 + i) for i in range(4)]
        B_pT = B_pg[16:20]
        kc = [page(20, 3), page(23, 3)]
        B_kc = [B_pg[20:23], B_pg[23:26]]
        vc = [page(26, 3), page(29, 3)]
        B_vc = [B_pg[26:29], B_pg[29:32]]
        cgT = [page(16 + c) for c in range(8)]
        B_cg = B_pg[16:24]
        mrg = [page(24 + c) for c in range(8)]
        B_mrg = B_pg[24:32]
        chunk_ctr = [0]
        S_wsxA = [mksem(f"s_wsxa{i}") for i in range(4)]
        S_wsxB = [mksem(f"s_wsxb{i}") for i in range(4)]

        def page_slots(p0, sems):
            out = []
            for k in range(4):
                flat = page(p0 + 2 * k, 2)
                out.append((flat, flat.rearrange("p (c n) -> p c n", c=DC), B_pg[p0 + 2 * k:p0 + 2 * k + 2], sems[k]))
            return out

        extraA = page_slots(24, S_wsxA)
        extraB = page_slots(16, S_wsxB)

        def proj(sec, co, b, bank, edge=False):
            w, wb = load_ws(wins_d[SEC[sec] * 8 + co, :, :])
            for c in range(DC):
                mm(psum[bank][:], w[:, c, :], xm[:, c, 1 + b * TB:1 + (b + 1) * TB], c == 0, c == DC - 1, wb + [B_xm[b]], [B_ps[bank]])
            return w, wb

        def attention_block(b):
            set_ws(extraA)
            ri = load_rope(b)
            for h in range(8):
                qa, qb_ = (0, 1) if h % 2 == 0 else (2, 3)
                proj("q", h, b, qa)
                proj("qp", h, b, qb_)
                rope_evac(qa, qb_, qT[h], [B_q[h]], ri)
            conv_branch(b)
            deferred = []
            for h in range(8):
                tiles = []
                chunks = []
                for r_ in range(4):
                    for half in range(2):
                        s_ = chunk_ctr[0] % 2
                        chunk_ctr[0] += 1
                        if half == 0:
                            ncol, j0, nj = 1024, 0, 8
                        else:
                            ncol, j0, nj = 1024 + NCTX, 8, 9
                        chunks.append((s_, r_, half * 1024, ncol, j0, nj))
                        for jj in range(nj):
                            nk = NCTX if (half == 1 and jj == nj - 1) else 128
                            tiles.append((s_, jj, nk, len(chunks) - 1, jj == 0))

                def load_chunk(k):
                    s_, r_, c0, ncol, j0, nj = chunks[k]
                    dma("sp", kc[s_][:, 0:ncol], kg_d[h][r_ * P:(r_ + 1) * P, c0:c0 + ncol], r=[B_kg[h]], w=B_kc[s_], sem=S_kc[s_])
                    dma("sp", vc[s_][:, 0:nj * 128], vg_d[h][r_ * P:(r_ + 1) * P, j0 * 128:(j0 + nj) * 128],
                        r=[B_vg[h]], w=B_vc[s_], sem=S_vc[s_])

                load_chunk(0)
                nt = len(tiles)

                def s_mm(i):
                    s_, jj, nk, _, _ = tiles[i]
                    ba, bb = (i % 2) * 2, (i % 2) * 2 + 1
                    mm(psum[ba][0:nk, :], kc[s_][0:64, jj * 128:jj * 128 + nk], qT[h][0:64, :], True, True, B_kc[s_] + [B_q[h]], [B_ps[ba]])
                    mm(psum[bb][0:nk, :], kc[s_][64:128, jj * 128:jj * 128 + nk], qT[h][64:128, :], True, True, B_kc[s_] + [B_q[h]], [B_ps[bb]])

                s_mm(0)
                for i in range(nt):
                    s_, jj, nk, ck, fst = tiles[i]
                    if fst and ck + 1 < len(chunks):
                        load_chunk(ck + 1)
                    ba, bb = (i % 2) * 2, (i % 2) * 2 + 1
                    pa, pb = (i % 2) * 2, (i % 2) * 2 + 1
                    act(pT[pa][0:nk, :], psum[ba][0:nk, :], AF.Exp, r=[B_ps[ba]], w=[B_pT[pa]], scale=0.125)
                    act(pT[pb][0:nk, :], psum[bb][0:nk, :], AF.Exp, r=[B_ps[bb]], w=[B_pT[pb]], scale=0.125)
                    if i + 1 < nt:
                        s_mm(i + 1)
                    first, last = i == 0, i == nt - 1
                    vt = vc[s_][0:nk, jj * 128:(jj + 1) * 128]
                    mm(psum[4][:], vt, pT[pa][0:nk, :], first, last, B_vc[s_] + [B_pT[pa]], [B_ps[4]])
                    mm(psum[5][:], vt, pT[pb][0:nk, :], first, last, B_vc[s_] + [B_pT[pb]], [B_ps[5]])
                    mm(psum[6][:], ones_b[0:nk, :], pT[pa][0:nk, :], first, last, [B_misc, B_pT[pa]], [B_ps[6]])
                    if first:
                        pg.emit("dve", lambda e, o=accs[1][:], i_=pT[pb][:]: e.tensor_copy(out=o, in_=i_), r=[B_pT[pb]], w=[B_acc[1]])
                    else:
                        tt(accs[1][0:nk, :], accs[1][0:nk, :], pT[pb][0:nk, :], ADD, r=[B_acc[1], B_pT[pb]], w=[B_acc[1]])
                    if deferred and i in (1, 3, 5, 7, 9, 11):
                        deferred.pop(0)()
                mm(psum[7][:], ones_f[:], accs[1][:], True, True, [B_misc, B_acc[1]], [B_ps[7]])
                r0, r0b = tmp()
                act(r0[:, 0:TB], psum[6][:], AF.Ln, r=[B_ps[6]], w=[r0b])
                o0, o0b = tmp()
                o1, o1b = tmp()
                pg.emit("dve", lambda e, o=o0[:, 0:TB], i_=psum[4][:]: e.tensor_copy(out=o, in_=i_), r=[B_ps[4]], w=[o0b])
                pg.emit("dve", lambda e, o=o1[:, 0:TB], i_=psum[5][:]: e.tensor_copy(out=o, in_=i_), r=[B_ps[5]], w=[o1b])
                r1, r1b = tmp()
                deferred.extend(norm_groups(h, r0, r0b, r1, r1b, o0, o0b, o1, o1b))
            while deferred:
                deferred.pop(0)()

        def norm_groups(h, r0, r0b, r1, r1b, o0, o0b, o1, o1b):
            st = {}

            def g1():
                act(r0[:, 0:TB], r0[:, 0:TB], AF.Exp, r=[r0b], w=[r0b], scale=-1.0)
                act(r1[:, 0:TB], psum[7][:], AF.Ln, r=[B_ps[7]], w=[r1b])
                act(r1[:, 0:TB], r1[:, 0:TB], AF.Exp, r=[r1b], w=[r1b], scale=-1.0)

            def g2():
                tt(o0[:, 0:TB], o0[:, 0:TB], r0[:, 0:TB], MUL, r=[o0b, r0b], w=[o0b])
                tt(o1[:, 0:TB], o1[:, 0:TB], r1[:, 0:TB], MUL, r=[o1b, r1b], w=[o1b])
                stt(o0[:, 0:TB], o1[:, 0:TB], nlam[:, 0:1], o0[:, 0:TB], MUL, ADD, r=[o1b, o0b, B_misc], w=[o0b])

            def g3():
                st["sq"] = tmp()
                sq, sqb = st["sq"]
                act(sq[:, 0:TB], o0[:, 0:TB], AF.Square, r=[o0b], w=[sqb])

            def g4():
                sq, sqb = st["sq"]
                mm(psum[7][:], ones_f[:], sq[:, 0:TB], True, True, [B_misc, sqb], [B_ps[7]])

            def g5():
                st["rs"] = stats_rstd(7, TB, 128.0)

            def g6():
                rs, rsb = st["rs"]
                stt(onT[h], o0[:, 0:TB], subg2[:, 0:1], rs[:, 0:TB], MUL, MUL, r=[o0b, B_misc, rsb], w=[B_on[h]])

            return [g1, g2, g3, g4, g5, g6]

        def conv_branch(b):
            lb = B_xm[b - 1] if b > 0 else B_xmL
            rb = B_xm[b + 1] if b + 1 < NB else B_xmR
            c_l, c_r = b * TB, b * TB + TB + 1
            for cc in range(DC):
                for si, sec in enumerate(("c", "x")):
                    w, wb = load_ws(wins_d[SEC[sec] * 8 + cc, :, :])
                    o = (cc * 2 + si) * 2
                    for c in range(DC):
                        mm(psum[4][:, o:o + 1], w[:, c, :], xm[:, c, c_l:c_l + 1], c == 0, c == DC - 1, wb + [lb], [B_ps[4]])
                    for c in range(DC):
                        mm(psum[4][:, o + 1:o + 2], w[:, c, :], xm[:, c, c_r:c_r + 1], c == 0, c == DC - 1, wb + [rb], [B_ps[4]])
            pe_v = psum[4][:, 0:32].rearrange("p (c s e) -> p c s e", c=DC, s=2)
            pg.emit("act", lambda e: e.activation(out=zedge[:], in_=pe_v, func=AF.Identity), r=[B_ps[4]], w=[B_ze])
            tt(zedge2[:], zedge[:, :, 0, :], zedge[:, :, 1, :], MUL, r=[B_ze], w=[B_ze2])
            for cc in range(DC):
                bc_, bx_, bb_ = (0, 1, 2) if cc % 2 == 0 else (3, 5, 6)
                proj("c", cc, b, bc_)
                proj("x", cc, b, bx_)
                proj("b", cc, b, bb_)
                t, tb = tmp()
                z, zb = tmp()
                act(t[:, 0:TB], psum[bc_][:], AF.Identity, r=[B_ps[bc_]], w=[tb])
                tt(z[:, 1:TB + 1], psum[bx_][:], t[:, 0:TB], MUL, r=[B_ps[bx_], tb], w=[zb])
                pg.emit("dve", lambda e, z=z, cc=cc: e.tensor_copy(out=z[:, 0:1], in_=zedge2[:, cc, 0:1]), r=[B_ze2, zb], w=[zb])
                pg.emit("dve", lambda e, z=z, cc=cc: e.tensor_copy(out=z[:, TB + 1:TB + 2], in_=zedge2[:, cc, 1:2]), r=[B_ze2, zb], w=[zb])
                a, ab = tmp()
                ts(a[:, 0:TB], z[:, 1:TB + 1], convw[:, 1, cc:cc + 1], MUL, r=[zb, B_const], w=[ab])
                stt(a[:, 0:TB], z[:, 0:TB], convw[:, 0, cc:cc + 1], a[:, 0:TB], MUL, ADD, r=[zb, B_const, ab], w=[ab])
                stt(a[:, 0:TB], z[:, 2:TB + 2], convw[:, 2, cc:cc + 1], a[:, 0:TB], MUL, ADD, r=[zb, B_const, ab], w=[ab])
                tt(cgT[cc], psum[bb_][:], a[:, 0:TB], MUL, r=[B_ps[bb_], ab], w=[B_cg[cc]])
            for co in range(DC):
                w, wb = load_ws(wpb_d[co, :, :])
                by_, bg_ = (0, 1) if co % 2 == 0 else (2, 3)
                for c in range(DC):
                    mm(psum[by_][:], w[:, c, :], cgT[c], c == 0, c == DC - 1, wb + [B_cg[c]], [B_ps[by_]])
                proj("gb", co, b, bg_)
                t, tb = tmp()
                act(t[:, 0:TB], psum[bg_][:], AF.Sigmoid, r=[B_ps[bg_]], w=[tb])
                tt(yT[:, co, :], psum[by_][:], t[:, 0:TB], MUL, r=[B_ps[by_], tb], w=[B_y[co]])

        def merge_out(b):
            set_ws(extraB)
            for co in range(DC):
                w, wb = load_ws(wpa_d[co, :, :])
                by_, bg_ = (0, 1) if co % 2 == 0 else (2, 3)
                for h in range(8):
                    mm(psum[by_][:], w[:, h, :], onT[h], h == 0, h == 7, wb + [B_on[h]], [B_ps[by_]])
                proj("ga", co, b, bg_)
                t, tb = tmp()
                act(t[:, 0:TB], psum[bg_][:], AF.Sigmoid, r=[B_ps[bg_]], w=[tb])
                t2, t2b = tmp()
                tt(t2[:, 0:TB], psum[by_][:], t[:, 0:TB], MUL, r=[B_ps[by_], tb], w=[t2b])
                tt(mrg[co], t2[:, 0:TB], yT[:, co, :], ADD, r=[t2b, B_y[co]], w=[B_mrg[co]], eng="dve")
            pend = None
            for co in range(DC):
                w, wb = load_ws(wo_d[co, :, :])
                bank = 4 + (co % 2)
                for c in range(DC):
                    mm(psum[bank][:], w[:, c, :], mrg[c], c == 0, c == DC - 1, wb + [B_mrg[c]], [B_ps[bank]])
                if pend is not None:
                    pend()
                pend = y_evac(bank, co, TB)
            pend()
            postnorm_update(lat[b], 5)

        for b in range(NB):
            attention_block(b)
            merge_out(b)

    set_ws(())
    if stage >= 3:
        ffn_layer([[lat[0], lat[1]], [lat[2], lat[3]]], 1, 6, 8)

    for b in range(NB):
        dma("sp", out_d.ap().rearrange("(c p) t -> p c t", p=P)[:, :, b * TB:(b + 1) * TB], hT[:, :, b * TB:(b + 1) * TB],
            r=[B_h[c][b] for c in range(DC)], w=[B_out[b]], sem=S_out)
    pg.final.append(S_out)

    pg.finalize(esems)
    with nc.Block() as block:
        @block.tensor
        def _(e):
            pg.replay("pe", e)

        @block.scalar
        def _(e):
            pg.replay("act", e)

        @block.vector
        def _(e):
            pg.replay("dve", e)

        @block.gpsimd
        def _(e):
            pg.replay("pool", e)

        @block.sync
        def _(e):
            pg.replay("sp", e)
    es.close()
    return nc


def _panels(W, pw):
    K, N = W.shape
    kc, npn = K // P, N // pw
    return np.ascontiguousarray(W.reshape(kc, P, npn, pw).transpose(2, 1, 0, 3).reshape(npn, P, kc * pw))


def _chunked(vec):
    return np.ascontiguousarray(vec.reshape(-1, P).T)


def _rope_tables(t0):
    pos = np.arange(t0, t0 + T)
    row = (pos // 64).astype(np.float32)
    col = (pos % 64).astype(np.float32)
    inv_freq = (np.float32(10000.0) ** (np.float32(-2.0) * np.arange(16, dtype=np.float32) / np.float32(32))).astype(np.float32)
    ang = np.zeros((T, 64), np.float32)
    sgn = np.zeros((64,), np.float32)
    for d in range(64):
        axis, half, fr = d // 32, (d % 32) // 16, d % 16
        ang[:, d] = (row if axis == 0 else col) * inv_freq[fr]
        sgn[d] = -1.0 if half == 0 else 1.0
    cos = np.cos(ang).astype(np.float32)
    sin = (np.sin(ang).astype(np.float32)) * sgn[None, :]
    cT = np.ascontiguousarray(np.concatenate([cos.T, cos.T], 0))
    sT = np.ascontiguousarray(np.concatenate([sin.T, sin.T], 0))
    return cT, sT


def _perm_cols():
    idx = np.arange(1024)
    d = idx % 64
    half = (d % 32) // 16
    return np.where(half == 0, idx + 16, idx - 16)


_CACHE = {}


def kernel(x, c, ctx, c_ctx, w_mod, b_mod, ffn1_pre_g, ffn1_post_g, ffn1_w_gate, ffn1_w_up, ffn1_w_down,
           mix_pre_g, mix_post_g, w_in, lam_q1, lam_k1, lam_q2, lam_k2, attn_subln_g, conv_w,
           w_attn_proj, w_conv_proj, w_out, ffn2_pre_g, ffn2_post_g, ffn2_w_gate, ffn2_w_up, ffn2_w_down, _stage=3):
    f = lambda a: np.asarray(a, dtype=np.float32)
    x, c, ctx, c_ctx = f(x), f(c), f(ctx), f(c_ctx)
    w_in0 = f(w_in)[0]
    perm = _perm_cols()
    secs = {"q": w_in0[:, 0:1024], "k": w_in0[:, 1024:2048], "b": w_in0[:, 3072:4096], "c": w_in0[:, 4096:5120],
            "x": w_in0[:, 5120:6144], "ga": w_in0[:, 6144:7168], "gb": w_in0[:, 7168:8192]}
    secs["qp"] = secs["q"][:, perm]
    secs["kp"] = secs["k"][:, perm]
    wins = np.concatenate([_panels(secs[k], 128) for k in ("q", "qp", "k", "kp", "b", "c", "x", "ga", "gb")], 0)
    winv = _panels(w_in0[:, 2048:3072], 256)
    shared = {
        "wmod": _panels(f(w_mod)[0], 128),
        "bmod": np.ascontiguousarray(f(b_mod)[0].reshape(72, P).T),
        "gains": np.ascontiguousarray(np.stack([_chunked(f(g)[0]) for g in
                                                (ffn1_pre_g, ffn1_post_g, mix_pre_g, mix_post_g, ffn2_pre_g, ffn2_post_g)], 1)),
        "subg": np.ascontiguousarray(f(attn_subln_g)[0].reshape(P, 1)),
        "convw": np.ascontiguousarray(np.stack([_chunked(f(conv_w)[0][j]) for j in range(3)], 1)),
        "lamv": np.ascontiguousarray(np.broadcast_to(np.stack([f(lam_q1)[0], f(lam_k1)[0], f(lam_q2)[0], f(lam_k2)[0]], 0)[None], (P, 4, 64))),
        "w1g": _panels(f(ffn1_w_gate)[0], 128), "w1u": _panels(f(ffn1_w_up)[0], 128), "w1d": _panels(f(ffn1_w_down)[0], 128),
        "w2g": _panels(f(ffn2_w_gate)[0], 128), "w2u": _panels(f(ffn2_w_up)[0], 128), "w2d": _panels(f(ffn2_w_down)[0], 128),
        "wins": wins, "winv": winv,
        "wpa": _panels(f(w_attn_proj)[0], 128), "wpb": _panels(f(w_conv_proj)[0], 128), "wo": _panels(f(w_out)[0], 128),
    }
    in_maps = []
    for core in range(8):
        b, r = core // 4, core % 4
        t0 = r * T
        cT, sT = _rope_tables(t0)
        sel = np.zeros((P, 8), np.float32)
        if r > 0:
            sel[:, r - 1] = 1.0
        if r < 3:
            sel[:, 4 + r + 1] = 1.0
        m = dict(shared)
        m["xT"] = np.ascontiguousarray(x[b, t0:t0 + T, :].T)
        m["ctxT"] = np.ascontiguousarray(ctx[b, r * NCTX:(r + 1) * NCTX, :].T)
        m["cvec"] = np.ascontiguousarray(np.stack([_chunked(c[b]), _chunked(c_ctx)], -1))
        m["ropec"], m["ropes"], m["sel"] = cT, sT, sel
        in_maps.append(m)
    if _stage not in _CACHE:
        _CACHE[_stage] = build(_stage)
    nc = _CACHE[_stage]
    res = run_bass_kernel_spmd(nc, in_maps, core_ids=list(range(8)))
    out = np.empty((2, 4 * T, 1024), np.float32)
    for core in range(8):
        b, r = core // 4, core % 4
        out[b, r * T:(r + 1) * T, :] = np.asarray(res.results[core]["outT"]).T
    return out
```

```python
import math
from contextlib import ExitStack
import numpy as np
import concourse.bass as bass
import concourse.mybir as mybir
from concourse.bass_utils import run_bass_kernel_spmd

F32 = mybir.dt.float32
BF16 = mybir.dt.bfloat16
AF = mybir.ActivationFunctionType
ALU = mybir.AluOpType

P = 128
T = 2048
TB = 512
NB = 4
DC = 8
FC = 22
NCTX = 64
KW = T + NCTX
NKT = 17
EPS = 1e-6
LAM_INIT = 0.8 - 0.6 * math.exp(-0.3 * 0)
GROUPS = [[0, 1, 2, 3], [4, 5, 6, 7]]
SEC = {"q": 0, "qp": 1, "k": 2, "kp": 3, "b": 4, "c": 5, "x": 6, "ga": 7, "gb": 8}


class Buf:
    __slots__ = ("name", "excl", "w", "rd")

    def __init__(self, name, excl=False):
        self.name = name
        self.excl = excl
        self.w = None
        self.rd = []


class Sem:
    def __init__(self, h):
        self.h = h
        self.count = 0


class Op:
    __slots__ = ("eng", "fn", "deps", "signal", "val", "sem", "amt")

    def __init__(self, eng, fn):
        self.eng = eng
        self.fn = fn
        self.deps = ()
        self.signal = False
        self.val = 0
        self.sem = None
        self.amt = 1


class Prog:
    ENGS = ("pe", "act", "dve", "pool", "sp")

    def __init__(self):
        self.ops = {e: [] for e in self.ENGS}
        self.final = []

    def emit(self, eng, fn, r=(), w=(), sem=None, amt=16):
        op = Op(eng, fn)
        deps = set()
        for b in r:
            if b.w is not None:
                deps.add(b.w)
            if b.excl:
                for o in b.rd:
                    if o.eng != eng:
                        deps.add(o)
        for b in w:
            if b.w is not None:
                deps.add(b.w)
            deps.update(b.rd)
        if eng == "pe":
            deps = {d for d in deps if d.eng != "pe"}
        for d in deps:
            d.signal = True
        op.deps = deps
        for b in r:
            b.rd.append(op)
        for b in w:
            b.w = op
            b.rd = []
        if sem is not None:
            sem.count += amt
            op.sem, op.val, op.amt, op.signal = sem, sem.count, amt, True
        self.ops[eng].append(op)
        return op

    def finalize(self, esems):
        for e in self.ENGS:
            s = esems[e]
            for op in self.ops[e]:
                if op.sem is None and op.signal:
                    s.count += 1
                    op.sem, op.val, op.amt = s, s.count, 1

    def replay(self, eng, e):
        waited = {}
        for op in self.ops[eng]:
            need = {}
            for d in op.deps:
                if need.get(d.sem, 0) < d.val:
                    need[d.sem] = d.val
            for s, v in need.items():
                if waited.get(s, 0) < v:
                    e.wait_ge(s.h, v)
                    waited[s] = v
            ins = op.fn(e)
            if op.signal:
                ins.then_inc(op.sem.h, op.amt)
        if eng == "sp":
            for s in self.final:
                e.wait_ge(s.h, s.count)


def build(stage=3):
    nc = bass.Bass("TRN2", target_bir_lowering=False)
    es = ExitStack()

    def din(name, shape, dt=F32):
        return nc.dram_tensor(name, list(shape), dt, kind="ExternalInput")

    xT_d = din("xT", [1024, T])
    ctxT_d = din("ctxT", [1024, NCTX])
    cvec_d = din("cvec", [P, DC, 2])
    wmod_d = din("wmod", [72, P, DC * 128])
    bmod_d = din("bmod", [P, 72])
    gains_d = din("gains", [P, 6, DC])
    subg_d = din("subg", [P, 1])
    convw_d = din("convw", [P, 3, DC])
    lamv_d = din("lamv", [P, 4, 64])
    ropec_d = din("ropec", [P, T])
    ropes_d = din("ropes", [P, T])
    sel_d = din("sel", [P, 8])
    ffw_d = []
    for i in (1, 2):
        ffw_d.append((din(f"w{i}g", [FC, P, DC * 128]), din(f"w{i}u", [FC, P, DC * 128]),
                      din(f"w{i}d", [DC, P, FC * 128])))
    wins_d = din("wins", [72, P, DC * 128])
    winv_d = din("winv", [4, P, DC * 256])
    wpa_d = din("wpa", [DC, P, DC * 128])
    wpb_d = din("wpb", [DC, P, DC * 128])
    wo_d = din("wo", [DC, P, DC * 128])
    out_d = nc.dram_tensor("outT", [1024, T], F32, kind="ExternalOutput")

    kb_d = [nc.dram_tensor(f"kb{h}", [P, KW], BF16) for h in range(8)]
    kg_d = [nc.dram_tensor(f"kg{h}", [4 * P, KW], BF16) for h in range(8)]
    vb_d = [nc.dram_tensor(f"vb{h}", [P, NKT * 128], BF16) for h in range(8)]
    vg_d = [nc.dram_tensor(f"vg{h}", [4 * P, NKT * 128], BF16) for h in range(8)]
    hb_d = nc.dram_tensor("hb", [P, 16], BF16)
    hg_d = nc.dram_tensor("hg", [4 * P, 16], BF16)

    def sb(name, shape, dt):
        return es.enter_context(nc.sbuf_tensor(name, list(shape), dt))

    hT = sb("hT", [P, DC, T], F32)
    hc = sb("hc", [P, DC, NCTX], F32)
    xm = sb("xm", [P, DC, T + 2], BF16)
    xmc = sb("xmc", [P, DC, NCTX], BF16)
    arena = sb("arena", [P, 32 * 512], BF16)
    yT = sb("yT", [P, DC, TB], F32)
    NTMP = 4
    tmps = [sb(f"tmp{i}", [P, TB + 2], F32) for i in range(NTMP)]
    NWS = 4
    wsl = [sb(f"ws{i}", [P, DC, 128], BF16) for i in range(NWS)]
    wdl = [sb(f"wd{i}", [P, FC, 128], BF16) for i in range(2)]
    rstd_t = [sb(f"rstd{i}", [P, TB], F32) for i in range(1)]
    accs = [None, sb("acc1", [P, TB], F32)]
    ropeCs = [sb(f"ropeC{i}", [P, TB], F32) for i in range(2)]
    ropeSs = [sb(f"ropeS{i}", [P, TB], F32) for i in range(2)]
    cvec = sb("cvec_s", [P, DC, 2], F32)
    scv = sb("scv", [P, DC, 2], BF16)
    bmod = sb("bmod_s", [P, 72], F32)
    modv = sb("modv", [P, 72, 2], F32)
    gains = sb("gains_s", [P, 6, DC], F32)
    subg = sb("subg_s", [P, 1], F32)
    subg2 = sb("subg2", [P, 1], F32)
    convw = sb("convw_s", [P, 3, DC], F32)
    lamv = sb("lamv_s", [P, 4, 64], F32)
    lamt = sb("lamt", [P, 2, 64], F32)
    lams = sb("lams", [P, 4], F32)
    nlam = sb("nlam", [P, 1], F32)
    sel = sb("sel_s", [P, 8], F32)
    coef = sb("coef", [P, 2, 9, DC], F32)
    ones_b = sb("ones_b", [P, P], BF16)
    ones_f = sb("ones_f", [P, P], F32)
    hgs = sb("hgs", [P, 4, 16], BF16)
    hst = sb("hst", [P, 2, DC], BF16)
    eps_t = sb("eps_t", [P, 1], F32)
    hacc = sb("hacc", [P, 2, DC], F32)
    zedge = sb("zedge", [P, DC, 2, 2], F32)
    zedge2 = sb("zedge2", [P, DC, 2], F32)
    psum = [es.enter_context(nc.psum_tensor(f"ps{i}", [P, TB], F32)) for i in range(8)]

    def mksem(name):
        return Sem(es.enter_context(nc.semaphore(name)))

    esems = {e: mksem("e_" + e) for e in Prog.ENGS}
    pg = Prog()

    B_h = [[Buf(f"h{c}_{b}") for b in range(NB)] for c in range(DC)]
    B_hc = Buf("hc")
    B_xm = [Buf(f"xm{b}") for b in range(NB)]
    B_xmL, B_xmR = Buf("xmL"), Buf("xmR")
    B_xmc = Buf("xmc")
    B_pg = [Buf(f"pg{i}") for i in range(32)]
    B_y = [Buf(f"y{c}") for c in range(DC)]
    B_tmp = [Buf(f"tmp{i}") for i in range(NTMP)]
    B_ws = [Buf(f"ws{i}") for i in range(NWS)]
    B_wd = [Buf(f"wd{i}") for i in range(2)]
    B_ps = [Buf(f"ps{i}", excl=True) for i in range(8)]
    B_ropeCs = [Buf("ropeC0"), Buf("ropeC1")]
    B_ropeSs = [Buf("ropeS0"), Buf("ropeS1")]
    B_ze, B_ze2, B_hacc, B_hst = Buf("ze"), Buf("ze2"), Buf("hacc"), Buf("hst")
    B_rstd = [Buf(f"rstd{i}") for i in range(2)]
    B_acc = [Buf(f"acc{i}") for i in range(2)]
    B_const = Buf("const")
    B_modv = Buf("modv")
    B_coef = Buf("coef")
    B_misc = Buf("misc")
    B_kb = [Buf(f"kb{h}") for h in range(8)]
    B_kg = [Buf(f"kg{h}") for h in range(8)]
    B_vb = [Buf(f"vb{h}") for h in range(8)]
    B_vg = [Buf(f"vg{h}") for h in range(8)]
    B_hb, B_hg, B_hgs = Buf("hb"), Buf("hg"), Buf("hgs")
    B_out = [Buf(f"out{b}") for b in range(NB)]

    S_ws = [mksem(f"s_ws{i}") for i in range(NWS)]
    S_wd = [mksem(f"s_wd{i}") for i in range(2)]
    S_x = [mksem(f"s_x{i}") for i in range(NB)]
    S_const = mksem("s_const")
    S_hc = mksem("s_hc")
    S_ropeCs = [mksem("s_ropeC0"), mksem("s_ropeC1")]
    S_ropeSs = [mksem("s_ropeS0"), mksem("s_ropeS1")]
    S_kc = [mksem(f"s_kc{i}") for i in range(2)]
    S_vc = [mksem(f"s_vc{i}") for i in range(2)]
    S_kb = [mksem(f"s_kb{h}") for h in range(8)]
    S_vb = [mksem(f"s_vb{h}") for h in range(8)]
    S_kg = [mksem(f"s_kg{h}") for h in range(8)]
    S_vg = [mksem(f"s_vg{h}") for h in range(8)]
    S_hb, S_hg, S_hgs = mksem("s_hb"), mksem("s_hg"), mksem("s_hgs")
    S_out = mksem("s_out")
    S_wv = [mksem(f"s_wv{i}") for i in range(2)]

    rot = {"tmp": 0, "ws": 0, "wd": 0, "rstd": 0, "ps": 0}

    def tmp():
        i = rot["tmp"]
        rot["tmp"] = (i + 1) % NTMP
        return tmps[i], B_tmp[i]

    def page(i, n=1):
        return arena[:, i * 512:(i + n) * 512]

    def mm(out, lhsT, rhs, start, stop, r, w):
        pg.emit("pe", lambda e, o=out, l=lhsT, rr=rhs, s=start, t=stop: e.matmul(o, lhsT=l, rhs=rr, start=s, stop=t),
                r=r, w=w)

    def act(out, in_, func, r, w, scale=None, bias=None):
        kw = {}
        if scale is not None:
            kw["scale"] = scale
        if bias is not None:
            kw["bias"] = bias
        pg.emit("act", lambda e, o=out, i=in_, f=func, k=kw: e.activation(out=o, in_=i, func=f, **k), r=r, w=w)

    def tt(out, in0, in1, op, r, w, eng="dve"):
        pg.emit(eng, lambda e, o=out, a=in0, b=in1, p=op: e.tensor_tensor(out=o, in0=a, in1=b, op=p), r=r, w=w)

    def ts(out, in0, s1, op0, r, w, s2=None, op1=None, eng="dve"):
        if op1 is None:
            pg.emit(eng, lambda e, o=out, a=in0, s=s1, p=op0: e.tensor_scalar(out=o, in0=a, scalar1=s, scalar2=None, op0=p),
                    r=r, w=w)
        else:
            pg.emit(eng, lambda e, o=out, a=in0, s=s1, p=op0, ss=s2, pp=op1:
                    e.tensor_scalar(out=o, in0=a, scalar1=s, scalar2=ss, op0=p, op1=pp), r=r, w=w)

    def stt(out, in0, scalar, in1, op0, op1, r, w):
        pg.emit("dve", lambda e, o=out, a=in0, s=scalar, b=in1, p0=op0, p1=op1:
                e.scalar_tensor_tensor(out=o, in0=a, scalar=s, in1=b, op0=p0, op1=p1), r=r, w=w)

    def dma(q, out, in_, r, w, sem):
        return pg.emit(q, lambda e, o=out, i=in_: e.dma_start(out=o, in_=i), r=r, w=w, sem=sem, amt=16)

    ws_slots = [(wsl[i][:].rearrange("p c n -> p (c n)"), wsl[i], [B_ws[i]], S_ws[i]) for i in range(NWS)]
    ws_state = {"slots": ws_slots, "i": 0}

    def load_ws(src_ap):
        sl = ws_state["slots"]
        i = ws_state["i"] % len(sl)
        ws_state["i"] = i + 1
        flat, w3, bufs, sem = sl[i]
        dma("pool", flat, src_ap, r=[], w=bufs, sem=sem)
        return w3, bufs

    def set_ws(extra=()):
        ws_state["slots"] = ws_slots + list(extra)
        ws_state["i"] = 0

    for dst, src in ((cvec, cvec_d), (bmod, bmod_d), (gains, gains_d), (subg, subg_d), (convw, convw_d),
                     (lamv, lamv_d), (sel, sel_d)):
        dma("sp", dst[:], src.ap(), r=[], w=[B_const], sem=S_const)
    for b in range(NB):
        dma("sp", hT[:, :, b * TB:(b + 1) * TB],
            xT_d.ap().rearrange("(c p) t -> p c t", p=P)[:, :, b * TB:(b + 1) * TB],
            r=[], w=[B_h[c][b] for c in range(DC)], sem=S_x[b])
    dma("sp", hc[:], ctxT_d.ap().rearrange("(c p) t -> p c t", p=P), r=[], w=[B_hc], sem=S_hc)
    pg.emit("dve", lambda e: e.memset(ones_b[:], 1.0), r=[], w=[B_misc])
    pg.emit("dve", lambda e: e.memset(ones_f[:], 1.0), r=[B_misc], w=[B_misc])
    pg.emit("dve", lambda e: e.memset(eps_t[:], EPS), r=[B_misc], w=[B_misc])

    def rstd_tile():
        i = rot["rstd"]
        rot["rstd"] = 0
        return rstd_t[i], B_rstd[i]

    AX = mybir.AxisListType.X
    MUL, ADD, SUB = ALU.mult, ALU.add, ALU.subtract

    act(scv[:], cvec[:], AF.Silu, r=[B_const], w=[B_misc])
    B_modvs = [Buf(f"modv{i}") for i in range(3)]
    B_coefs = [Buf(f"coef{i}") for i in range(3)]

    def mod_part_steps(s_, bank):
        j0 = 24 * s_
        steps = []

        def panel(j):
            bk = bank + (j % 2)
            wt, wb = load_ws(wmod_d[j, :, :])
            for c in range(DC):
                mm(psum[bk][:, 0:2], wt[:, c, :], scv[:, c, :], c == 0, c == DC - 1, wb + [B_misc], [B_ps[bk]])
            ts(modv[:, j, :], psum[bk][:, 0:2], bmod[:, j:j + 1], ADD, r=[B_ps[bk], B_const], w=[B_modvs[s_]])

        def final():
            half = 0.5 if s_ != 1 else 1.0
            for v in range(2):
                stt(coef[:, v, 3 * s_, :], modv[:, (3 * s_ + 1) * 8:(3 * s_ + 2) * 8, v], 1.0, gains[:, 2 * s_, :], ADD, MUL,
                    r=[B_modvs[s_], B_const], w=[B_coefs[s_]])
                ts(coef[:, v, 3 * s_ + 1, :], modv[:, (3 * s_) * 8:(3 * s_ + 1) * 8, v], 1.0, MUL, r=[B_modvs[s_]], w=[B_coefs[s_]])
                stt(coef[:, v, 3 * s_ + 2, :], modv[:, (3 * s_ + 2) * 8:(3 * s_ + 3) * 8, v], half, gains[:, 2 * s_ + 1, :], MUL, MUL,
                    r=[B_modvs[s_], B_const], w=[B_coefs[s_]])

        for j in range(j0, j0 + 24):
            steps.append(lambda j=j: panel(j))
        steps.append(final)
        return steps

    def mod_part(s_, bank):
        for st_ in mod_part_steps(s_, bank):
            st_()

    pending_mod = []

    mod_part(0, 0)
    tt(lamt[:, 0, :], lamv[:, 0, :], lamv[:, 1, :], MUL, r=[B_const], w=[B_misc])
    tt(lamt[:, 1, :], lamv[:, 2, :], lamv[:, 3, :], MUL, r=[B_const, B_misc], w=[B_misc])
    pg.emit("dve", lambda e: e.tensor_reduce(out=lams[:, 0:2], in_=lamt[:], axis=AX, op=ADD), r=[B_misc], w=[B_misc])
    act(lams[:, 2:4], lams[:, 0:2], AF.Exp, r=[B_misc], w=[B_misc])
    stt(nlam[:], lams[:, 3:4], -LAM_INIT, lams[:, 2:3], ADD, SUB, r=[B_misc], w=[B_misc])
    ts(subg2[:], subg[:], 1.0 - LAM_INIT, MUL, r=[B_const, B_misc], w=[B_misc])

    class Blk:
        def __init__(self, idx):
            self.idx = idx
            if idx >= 0:
                self.n, self.v = TB, 0
                t0 = idx * TB
                self.h = lambda c: hT[:, c, t0:t0 + TB]
                self.hb = lambda c: B_h[c][idx]
            else:
                self.n, self.v = NCTX, 1
                self.h = lambda c: hc[:, c, :]
                self.hb = lambda c: B_hc

    def stats_rstd(bank, n, nfeat):
        rs, rsb = rstd_tile()
        act(rs[:, 0:n], psum[bank][:, 0:n], AF.Ln, r=[B_ps[bank]], w=[rsb], scale=1.0 / nfeat, bias=eps_t[:, 0:1])
        act(rs[:, 0:n], rs[:, 0:n], AF.Exp, r=[rsb], w=[rsb], scale=-0.5)
        return rs, rsb

    xq_bufs = {}

    def prenorm(blk, kA, xh, xb):
        n, v = blk.n, blk.v
        for c in range(DC):
            qb = xq_bufs.setdefault((id(xb), c), Buf("xq"))
            if c == 0:
                act(xh(c), blk.h(c), AF.Square, r=[blk.hb(c)], w=[xb, qb])
            else:
                act(xh(c), blk.h(c), AF.Square, r=[blk.hb(c), xb], w=[qb])
            mm(psum[6][:, 0:n], ones_b[:], xh(c), c == 0, c == DC - 1, [B_misc, qb], [B_ps[6]])
        rs, rsb = stats_rstd(6, n, 1024.0)
        for c in range(DC):
            qb = xq_bufs[(id(xb), c)]
            t, tb = tmp()
            stt(t[:, 0:n], blk.h(c), coef[:, v, kA, c:c + 1], rs[:, 0:n], MUL, MUL, r=[blk.hb(c), B_coefs[kA // 3], rsb], w=[tb])
            act(xh(c), t[:, 0:n], AF.Identity, r=[tb, B_coefs[kA // 3]], w=[xb, qb], bias=coef[:, v, kA + 1, c:c + 1])

    def postnorm_update(blk, kB):
        n, v = blk.n, blk.v
        rs, rsb = stats_rstd(7, n, 1024.0)
        for c in range(DC):
            t, tb = tmp()
            stt(t[:, 0:n], yT[:, c, 0:n], coef[:, v, kB, c:c + 1], rs[:, 0:n], MUL, MUL, r=[B_y[c], B_coefs[kB // 3], rsb], w=[tb])
            tt(blk.h(c), blk.h(c), t[:, 0:n], ADD, r=[blk.hb(c), tb], w=[blk.hb(c)], eng="dve")

    def y_evac(bank, c, n):
        pg.emit("dve", lambda e, o=yT[:, c, 0:n], i=psum[bank][:, 0:n]: e.tensor_copy(out=o, in_=i), r=[B_ps[bank]], w=[B_y[c]])
        t, tb = tmp()
        act(t[:, 0:n], yT[:, c, 0:n], AF.Square, r=[B_y[c]], w=[tb])
        return lambda: mm(psum[7][:, 0:n], ones_f[:], t[:, 0:n], c == 0, c == DC - 1, [B_misc, tb], [B_ps[7]])

    B_pgx = [Buf(f"pgx{i}") for i in range(16)]
    bank_ctr = [0]

    def act_page(f, si, n, is_ctx):
        if is_ctx:
            idx, c0 = 44 + f // 8, (f % 8) * NCTX
        else:
            idx, c0 = 2 * f + si, 0
        if idx < 32:
            return arena[:, idx * 512 + c0:idx * 512 + c0 + n], [B_pg[idx]]
        i = idx - 32
        col = 1025 + (i // 8) * 512 + c0
        return xm[:, i % 8, col:col + n], [B_pgx[i], B_xm[2 + i // 8]]

    def xhat_of(blk, si):
        if blk.idx >= 0:
            return (lambda c: xm[:, c, 1 + si * TB:1 + (si + 1) * TB]), B_xm[si]
        return (lambda c: xmc[:, c, :]), B_xmc

    def ffn_super(blocks, wg_d, wu_d, wd_d, kB, between=None):
        for f in range(FC):
            wg, wgb = load_ws(wg_d[f, :, :])
            wu, wub = load_ws(wu_d[f, :, :])
            for si, blk in enumerate(blocks):
                n = blk.n
                xh, xb = xhat_of(blk, si)
                k = bank_ctr[0] % 2
                bank_ctr[0] += 1
                gb_, ub_ = k, 2 + k
                for c in range(DC):
                    mm(psum[gb_][:, 0:n], wg[:, c, :], xh(c), c == 0, c == DC - 1, wgb + [xb], [B_ps[gb_]])
                for c in range(DC):
                    mm(psum[ub_][:, 0:n], wu[:, c, :], xh(c), c == 0, c == DC - 1, wub + [xb], [B_ps[ub_]])
                t, tb = tmp()
                act(t[:, 0:n], psum[gb_][:, 0:n], AF.Silu, r=[B_ps[gb_]], w=[tb])
                ap_, bufs_ = act_page(f, si, n, blk.idx < 0)
                tt(ap_, psum[ub_][:, 0:n], t[:, 0:n], MUL, r=[B_ps[ub_], tb], w=bufs_)
        if between is not None:
            between()
        for si, blk in enumerate(blocks):
            n = blk.n
            pend = None
            for co in range(DC):
                i = rot["wd"]
                rot["wd"] = (i + 1) % 2
                dma("pool", wdl[i][:].rearrange("p f n -> p (f n)"), wd_d[co, :, :], r=[], w=[B_wd[i]], sem=S_wd[i])
                bank = 4 + (co % 2)
                for f in range(FC):
                    ap_, bufs_ = act_page(f, si, n, blk.idx < 0)
                    mm(psum[bank][:, 0:n], wdl[i][:, f, :], ap_, f == 0, f == FC - 1, [B_wd[i]] + bufs_, [B_ps[bank]])
                if pend is not None:
                    pend()
                pend = y_evac(bank, co, n)
                for _ in range(2):
                    if pending_mod:
                        pending_mod.pop(0)()
            pend()
            postnorm_update(blk, kB)

    def ffn_layer(supers, widx, kA, kB, after_first=None):
        wg_d, wu_d, wd_d = ffw_d[widx]

        def prenorm_super(blocks):
            for si, blk in enumerate(blocks):
                xh, xb = xhat_of(blk, si)
                prenorm(blk, kA, xh, xb)

        prenorm_super(supers[0])
        for i, blocks in enumerate(supers):
            def between(i=i):
                if i == 0 and after_first is not None:
                    after_first()
                if i + 1 < len(supers):
                    prenorm_super(supers[i + 1])
            ffn_super(blocks, wg_d, wu_d, wd_d, kB, between)

    lat = [Blk(b) for b in range(NB)]
    ctxb = Blk(-1)

    pending_mod.extend(mod_part_steps(1, 2) + mod_part_steps(2, 2))
    ffn_layer([[lat[0], lat[1], ctxb], [lat[2], lat[3]]], 0, 0, 2)
    while pending_mod:
        pending_mod.pop(0)()

    if stage >= 2:
        for b in range(NB):
            prenorm(lat[b], 3, (lambda c, b=b: xm[:, c, 1 + b * TB:1 + (b + 1) * TB]), B_xm[b])
        prenorm(ctxb, 3, (lambda c: xmc[:, c, :]), B_xmc)

        pg.emit("dve", lambda e: e.tensor_copy(out=hst[:, 0, :], in_=xm[:, :, 1]), r=[B_xm[0]], w=[B_hst])
        pg.emit("dve", lambda e: e.tensor_copy(out=hst[:, 1, :], in_=xm[:, :, T]), r=[B_xm[NB - 1], B_hst], w=[B_hst])
        dma("sp", hb_d.ap(), hst[:].rearrange("p s c -> p (s c)"), r=[B_hst], w=[B_hb], sem=S_hb)
        pg.emit("pool", lambda e: e.collective_compute("AllGather", ALU.bypass, replica_groups=GROUPS,
                                                       ins=[hb_d.ap().opt()], outs=[hg_d.ap().opt()]),
                r=[B_hb], w=[B_hg], sem=S_hg, amt=1)
        dma("sp", hgs[:], hg_d.ap().rearrange("(r p) n -> p r n", p=P), r=[B_hg], w=[B_hgs], sem=S_hgs)
        hgv = hgs[:].rearrange("p r (s c) -> p r s c", s=2)
        for side in range(2):
            src_s = 1 - side
            ts(hacc[:, side, :], hgv[:, 0, src_s, :], sel[:, side * 4:side * 4 + 1], MUL, r=[B_hgs, B_const], w=[B_hacc])
            for r_ in range(1, 4):
                stt(hacc[:, side, :], hgv[:, r_, src_s, :], sel[:, side * 4 + r_:side * 4 + r_ + 1], hacc[:, side, :], MUL, ADD,
                    r=[B_hgs, B_const, B_hacc], w=[B_hacc])
        pg.emit("dve", lambda e: e.tensor_copy(out=xm[:, :, 0], in_=hacc[:, 0, :]), r=[B_hacc], w=[B_xmL])
        pg.emit("dve", lambda e: e.tensor_copy(out=xm[:, :, T + 1], in_=hacc[:, 1, :]), r=[B_hacc], w=[B_xmR])

        rope_ctr = [0]
        ropeC_v = [ropeCs[0][:], ropeCs[1][:], page(23, 2).bitcast(F32), page(27, 2).bitcast(F32)]
        ropeS_v = [ropeSs[0][:], ropeSs[1][:], page(25, 2).bitcast(F32), page(29, 2).bitcast(F32)]
        B_ropeC_v = [[B_ropeCs[0]], [B_ropeCs[1]], B_pg[23:25], B_pg[27:29]]
        B_ropeS_v = [[B_ropeSs[0]], [B_ropeSs[1]], B_pg[25:27], B_pg[29:31]]
        S_rope_x = [mksem(f"s_ropex{i}") for i in range(4)]

        def load_rope(b, ri=None):
            if ri is None:
                ri = rope_ctr[0] % 2
                rope_ctr[0] += 1
            if ri < 2:
                sc_, ss_ = S_ropeCs[ri], S_ropeSs[ri]
            else:
                sc_, ss_ = S_rope_x[(ri - 2) * 2], S_rope_x[(ri - 2) * 2 + 1]
            dma("sp", ropeC_v[ri], ropec_d[:, b * TB:(b + 1) * TB], r=[], w=B_ropeC_v[ri], sem=sc_)
            dma("sp", ropeS_v[ri], ropes_d[:, b * TB:(b + 1) * TB], r=[], w=B_ropeS_v[ri], sem=ss_)
            return ri

        def rope_evac(bk_a, bk_p, out_ap, out_bufs, ri):
            t1, t1b = tmp()
            t2, t2b = tmp()
            tt(t1[:, 0:TB], psum[bk_a][:], ropeC_v[ri], MUL, r=[B_ps[bk_a]] + B_ropeC_v[ri], w=[t1b])
            tt(t2[:, 0:TB], psum[bk_p][:], ropeS_v[ri], MUL, r=[B_ps[bk_p]] + B_ropeS_v[ri], w=[t2b])
            tt(out_ap, t1[:, 0:TB], t2[:, 0:TB], ADD, r=[t1b, t2b], w=out_bufs, eng="dve")

        wvs = page(0, 4).rearrange("p (c n) -> p c n", c=DC)
        B_wvs = B_pg[0:4]
        vst = arena[:, 4 * 512:4 * 512 + 2 * NKT * 128].rearrange("p (h j e) -> p h j e", h=2, j=NKT)
        B_vst = B_pg[4:13]
        kst = [page(13, 5), page(18, 5)]
        B_kst = [B_pg[13:18], B_pg[18:23]]

        pending_cc = []

        def flush_cc():
            for fn_ in pending_cc:
                fn_()
            del pending_cc[:]

        def v_pair(hp):
            dma("pool", wvs.rearrange("p c n -> p (c n)"), winv_d[hp, :, :], r=[], w=B_wvs, sem=S_wv[0])
            for j in range(NKT):
                bank = j % 2
                nk = 128 if j < 16 else NCTX
                for c in range(DC):
                    lhsT = xm[:, c, 1 + j * 128:1 + (j + 1) * 128] if j < 16 else xmc[:, c, :]
                    rb = B_xm[j // 4] if j < 16 else B_xmc
                    mm(psum[bank][0:nk, 0:256], lhsT, wvs[:, c, :], c == 0, c == DC - 1, [rb] + B_wvs, [B_ps[bank]])
                act(vst[0:nk, :, j, :], psum[bank][0:nk, 0:256].rearrange("p (h e) -> p h e", h=2), AF.Identity,
                    r=[B_ps[bank]], w=B_vst)
            for hh in range(2):
                h = hp * 2 + hh
                dma("sp", vb_d[h].ap(), vst[:, hh, :, :].rearrange("p j e -> p (j e)"), r=B_vst, w=[B_vb[h]], sem=S_vb[h])
                pending_cc.append(lambda h=h: pg.emit("pool", lambda e, h=h: e.collective_compute(
                    "AllGather", ALU.bypass, replica_groups=GROUPS, ins=[vb_d[h].ap().opt()], outs=[vg_d[h].ap().opt()], dma_qos="P2"),
                    r=[B_vb[h]], w=[B_vg[h]], sem=S_vg[h], amt=1))

        def k_head(h):
            si = h % 2
            ks, ksb = kst[si], B_kst[si]
            wk, wkb = load_ws(wins_d[SEC["k"] * 8 + h, :, :])
            wp, wpb_ = load_ws(wins_d[SEC["kp"] * 8 + h, :, :])
            for b in range(NB):
                ri = b
                rhs_b = B_xm[b]
                ka, kp_ = (2, 3) if b % 2 == 0 else (4, 5)
                for c in range(DC):
                    mm(psum[ka][:], wk[:, c, :], xm[:, c, 1 + b * TB:1 + (b + 1) * TB], c == 0, c == DC - 1, wkb + [rhs_b], [B_ps[ka]])
                for c in range(DC):
                    mm(psum[kp_][:], wp[:, c, :], xm[:, c, 1 + b * TB:1 + (b + 1) * TB], c == 0, c == DC - 1, wpb_ + [rhs_b], [B_ps[kp_]])
                kb2, kb3 = (2, 3) if b % 2 == 0 else (4, 5)
                rope_evac(kb2, kb3, ks[:, b * TB:(b + 1) * TB], ksb, ri)
            for c in range(DC):
                mm(psum[2][:, 0:NCTX], wk[:, c, :], xmc[:, c, :], c == 0, c == DC - 1, wkb + [B_xmc], [B_ps[2]])
            act(ks[:, T:T + NCTX], psum[2][:, 0:NCTX], AF.Identity, r=[B_ps[2]], w=ksb)
            dma("sp", kb_d[h].ap(), ks[:, 0:KW], r=ksb, w=[B_kb[h]], sem=S_kb[h])
            flush_cc()
            pending_cc.append(lambda h=h: pg.emit("pool", lambda e, h=h: e.collective_compute(
                "AllGather", ALU.bypass, replica_groups=GROUPS, ins=[kb_d[h].ap().opt()], outs=[kg_d[h].ap().opt()], dma_qos="P2"),
                r=[B_kb[h]], w=[B_kg[h]], sem=S_kg[h], amt=1))

        for b in range(NB):
            load_rope(b, b)
        for hp in range(4):
            v_pair(hp)
            k_head(2 * hp)
            k_head(2 * hp + 1)
        flush_cc()

        qT = [page(h) for h in range(8)]
        B_q = B_pg[0:8]
        onT = [page(8 + h) for h in range(8)]
        B_on = B_pg[8:16]
        pT = [page(16 + i) for i in range(4)]
        B_pT = B_pg[16:20]
        kc = [page(20, 3), page(23, 3)]
        B_kc = [B_pg[20:23], B_pg[23:26]]
        vc = [page(26, 3), page(29, 3)]
        B_vc = [B_pg[26:29], B_pg[29:32]]
        cgT = [page(16 + c) for c in range(8)]
        B_cg = B_pg[16:24]
        mrg = [page(24 + c) for c in range(8)]
        B_mrg = B_pg[24:32]
        chunk_ctr = [0]
        S_wsxA = [mksem(f"s_wsxa{i}") for i in range(4)]
        S_wsxB = [mksem(f"s_wsxb{i}") for i in range(4)]

        def page_slots(p0, sems):
            out = []
            for k in range(4):
                flat = page(p0 + 2 * k, 2)
                out.append((flat, flat.rearrange("p (c n) -> p c n", c=DC), B_pg[p0 + 2 * k:p0 + 2 * k + 2], sems[k]))
            return out

        extraA = page_slots(24, S_wsxA)
        extraB = page_slots(16, S_wsxB)

        def proj(sec, co, b, bank, edge=False):
            w, wb = load_ws(wins_d[SEC[sec] * 8 + co, :, :])
            for c in range(DC):
                mm(psum[bank][:], w[:, c, :], xm[:, c, 1 + b * TB:1 + (b + 1) * TB], c == 0, c == DC - 1, wb + [B_xm[b]], [B_ps[bank]])
            return w, wb

        def attention_block(b):
            set_ws(extraA)
            ri = load_rope(b)
            for h in range(8):
                qa, qb_ = (0, 1) if h % 2 == 0 else (2, 3)
                proj("q", h, b, qa)
                proj("qp", h, b, qb_)
                rope_evac(qa, qb_, qT[h], [B_q[h]], ri)
            conv_branch(b)
            deferred = []
            for h in range(8):
                tiles = []
                chunks = []
                for r_ in range(4):
                    for half in range(2):
                        s_ = chunk_ctr[0] % 2
                        chunk_ctr[0] += 1
                        if half == 0:
                            ncol, j0, nj = 1024, 0, 8
                        else:
                            ncol, j0, nj = 1024 + NCTX, 8, 9
                        chunks.append((s_, r_, half * 1024, ncol, j0, nj))
                        for jj in range(nj):
                            nk = NCTX if (half == 1 and jj == nj - 1) else 128
                            tiles.append((s_, jj, nk, len(chunks) - 1, jj == 0))

                def load_chunk(k):
                    s_, r_, c0, ncol, j0, nj = chunks[k]
                    dma("sp", kc[s_][:, 0:ncol], kg_d[h][r_ * P:(r_ + 1) * P, c0:c0 + ncol], r=[B_kg[h]], w=B_kc[s_], sem=S_kc[s_])
                    dma("sp", vc[s_][:, 0:nj * 128], vg_d[h][r_ * P:(r_ + 1) * P, j0 * 128:(j0 + nj) * 128],
                        r=[B_vg[h]], w=B_vc[s_], sem=S_vc[s_])

                load_chunk(0)
                nt = len(tiles)

                def s_mm(i):
                    s_, jj, nk, _, _ = tiles[i]
                    ba, bb = (i % 2) * 2, (i % 2) * 2 + 1
                    mm(psum[ba][0:nk, :], kc[s_][0:64, jj * 128:jj * 128 + nk], qT[h][0:64, :], True, True, B_kc[s_] + [B_q[h]], [B_ps[ba]])
                    mm(psum[bb][0:nk, :], kc[s_][64:128, jj * 128:jj * 128 + nk], qT[h][64:128, :], True, True, B_kc[s_] + [B_q[h]], [B_ps[bb]])

                s_mm(0)
                for i in range(nt):
                    s_, jj, nk, ck, fst = tiles[i]
                    if fst and ck + 1 < len(chunks):
                        load_chunk(ck + 1)
                    ba, bb = (i % 2) * 2, (i % 2) * 2 + 1
                    pa, pb = (i % 2) * 2, (i % 2) * 2 + 1
                    act(pT[pa][0:nk, :], psum[ba][0:nk, :], AF.Exp, r=[B_ps[ba]], w=[B_pT[pa]], scale=0.125)
                    act(pT[pb][0:nk, :], psum[bb][0:nk, :], AF.Exp, r=[B_ps[bb]], w=[B_pT[pb]], scale=0.125)
                    if i + 1 < nt:
                        s_mm(i + 1)
                    first, last = i == 0, i == nt - 1
                    vt = vc[s_][0:nk, jj * 128:(jj + 1) * 128]
                    mm(psum[4][:], vt, pT[pa][0:nk, :], first, last, B_vc[s_] + [B_pT[pa]], [B_ps[4]])
                    mm(psum[5][:], vt, pT[pb][0:nk, :], first, last, B_vc[s_] + [B_pT[pb]], [B_ps[5]])
                    mm(psum[6][:], ones_b[0:nk, :], pT[pa][0:nk, :], first, last, [B_misc, B_pT[pa]], [B_ps[6]])
                    if first:
                        pg.emit("dve", lambda e, o=accs[1][:], i_=pT[pb][:]: e.tensor_copy(out=o, in_=i_), r=[B_pT[pb]], w=[B_acc[1]])
                    else:
                        tt(accs[1][0:nk, :], accs[1][0:nk, :], pT[pb][0:nk, :], ADD, r=[B_acc[1], B_pT[pb]], w=[B_acc[1]])
                    if deferred and i in (1, 3, 5, 7, 9, 11):
                        deferred.pop(0)()
                mm(psum[7][:], ones_f[:], accs[1][:], True, True, [B_misc, B_acc[1]], [B_ps[7]])
                r0, r0b = tmp()
                act(r0[:, 0:TB], psum[6][:], AF.Ln, r=[B_ps[6]], w=[r0b])
                o0, o0b = tmp()
                o1, o1b = tmp()
                pg.emit("dve", lambda e, o=o0[:, 0:TB], i_=psum[4][:]: e.tensor_copy(out=o, in_=i_), r=[B_ps[4]], w=[o0b])
                pg.emit("dve", lambda e, o=o1[:, 0:TB], i_=psum[5][:]: e.tensor_copy(out=o, in_=i_), r=[B_ps[5]], w=[o1b])
                r1, r1b = tmp()
                deferred.extend(norm_groups(h, r0, r0b, r1, r1b, o0, o0b, o1, o1b))
            while deferred:
                deferred.pop(0)()

        def norm_groups(h, r0, r0b, r1, r1b, o0, o0b, o1, o1b):
            st = {}

            def g1():
                act(r0[:, 0:TB], r0[:, 0:TB], AF.Exp, r=[r0b], w=[r0b], scale=-1.0)
                act(r1[:, 0:TB], psum[7][:], AF.Ln, r=[B_ps[7]], w=[r1b])
                act(r1[:, 0:TB], r1[:, 0:TB], AF.Exp, r=[r1b], w=[r1b], scale=-1.0)

            def g2():
                tt(o0[:, 0:TB], o0[:, 0:TB], r0[:, 0:TB], MUL, r=[o0b, r0b], w=[o0b])
                tt(o1[:, 0:TB], o1[:, 0:TB], r1[:, 0:TB], MUL, r=[o1b, r1b], w=[o1b])
                stt(o0[:, 0:TB], o1[:, 0:TB], nlam[:, 0:1], o0[:, 0:TB], MUL, ADD, r=[o1b, o0b, B_misc], w=[o0b])

            def g3():
                st["sq"] = tmp()
                sq, sqb = st["sq"]
                act(sq[:, 0:TB], o0[:, 0:TB], AF.Square, r=[o0b], w=[sqb])

            def g4():
                sq, sqb = st["sq"]
                mm(psum[7][:], ones_f[:], sq[:, 0:TB], True, True, [B_misc, sqb], [B_ps[7]])

            def g5():
                st["rs"] = stats_rstd(7, TB, 128.0)

            def g6():
                rs, rsb = st["rs"]
                stt(onT[h], o0[:, 0:TB], subg2[:, 0:1], rs[:, 0:TB], MUL, MUL, r=[o0b, B_misc, rsb], w=[B_on[h]])

            return [g1, g2, g3, g4, g5, g6]

        def conv_branch(b):
            lb = B_xm[b - 1] if b > 0 else B_xmL
            rb = B_xm[b + 1] if b + 1 < NB else B_xmR
            c_l, c_r = b * TB, b * TB + TB + 1
            for cc in range(DC):
                bc_, bx_, bb_ = (0, 1, 2) if cc % 2 == 0 else (3, 5, 6)
                be_ = 4 if cc % 2 == 0 else 7
                for si, (sec, bk_) in enumerate((("c", bc_), ("x", bx_))):
                    w, wb = proj(sec, cc, b, bk_)
                    for c in range(DC):
                        mm(psum[be_][:, si * 2:si * 2 + 2], w[:, c, :], xm[:, c, c_l:c_r + 1:TB + 1], c == 0, c == DC - 1,
                           wb + [lb, rb], [B_ps[be_]])
                proj("b", cc, b, bb_)
                pg.emit("act", lambda e, o=zedge[:, cc, :, :], i_=psum[be_][:, 0:4].rearrange("p (s e) -> p s e", s=2):
                        e.activation(out=o, in_=i_, func=AF.Identity), r=[B_ps[be_]], w=[B_ze])
                tt(zedge2[:, cc, :], zedge[:, cc, 0, :], zedge[:, cc, 1, :], MUL, r=[B_ze], w=[B_ze2])
                t, tb = tmp()
                z, zb = tmp()
                act(t[:, 0:TB], psum[bc_][:], AF.Identity, r=[B_ps[bc_]], w=[tb])
                tt(z[:, 1:TB + 1], psum[bx_][:], t[:, 0:TB], MUL, r=[B_ps[bx_], tb], w=[zb])
                pg.emit("dve", lambda e, z=z, cc=cc: e.tensor_copy(out=z[:, 0:1], in_=zedge2[:, cc, 0:1]), r=[B_ze2, zb], w=[zb])
                pg.emit("dve", lambda e, z=z, cc=cc: e.tensor_copy(out=z[:, TB + 1:TB + 2], in_=zedge2[:, cc, 1:2]), r=[B_ze2, zb], w=[zb])
                a, ab = tmp()
                ts(a[:, 0:TB], z[:, 1:TB + 1], convw[:, 1, cc:cc + 1], MUL, r=[zb, B_const], w=[ab])
                stt(a[:, 0:TB], z[:, 0:TB], convw[:, 0, cc:cc + 1], a[:, 0:TB], MUL, ADD, r=[zb, B_const, ab], w=[ab])
                stt(a[:, 0:TB], z[:, 2:TB + 2], convw[:, 2, cc:cc + 1], a[:, 0:TB], MUL, ADD, r=[zb, B_const, ab], w=[ab])
                tt(cgT[cc], psum[bb_][:], a[:, 0:TB], MUL, r=[B_ps[bb_], ab], w=[B_cg[cc]])
            for co in range(DC):
                w, wb = load_ws(wpb_d[co, :, :])
                by_, bg_ = (0, 1) if co % 2 == 0 else (2, 3)
                for c in range(DC):
                    mm(psum[by_][:], w[:, c, :], cgT[c], c == 0, c == DC - 1, wb + [B_cg[c]], [B_ps[by_]])
                proj("gb", co, b, bg_)
                t, tb = tmp()
                act(t[:, 0:TB], psum[bg_][:], AF.Sigmoid, r=[B_ps[bg_]], w=[tb])
                tt(yT[:, co, :], psum[by_][:], t[:, 0:TB], MUL, r=[B_ps[by_], tb], w=[B_y[co]])

        def merge_out(b):
            set_ws(extraB)
            for co in range(DC):
                w, wb = load_ws(wpa_d[co, :, :])
                by_, bg_ = (0, 1) if co % 2 == 0 else (2, 3)
                for h in range(8):
                    mm(psum[by_][:], w[:, h, :], onT[h], h == 0, h == 7, wb + [B_on[h]], [B_ps[by_]])
                proj("ga", co, b, bg_)
                t, tb = tmp()
                act(t[:, 0:TB], psum[bg_][:], AF.Sigmoid, r=[B_ps[bg_]], w=[tb])
                t2, t2b = tmp()
                tt(t2[:, 0:TB], psum[by_][:], t[:, 0:TB], MUL, r=[B_ps[by_], tb], w=[t2b])
                tt(mrg[co], t2[:, 0:TB], yT[:, co, :], ADD, r=[t2b, B_y[co]], w=[B_mrg[co]], eng="dve")
            pend = None
            for co in range(DC):
                w, wb = load_ws(wo_d[co, :, :])
                bank = 4 + (co % 2)
                for c in range(DC):
                    mm(psum[bank][:], w[:, c, :], mrg[c], c == 0, c == DC - 1, wb + [B_mrg[c]], [B_ps[bank]])
                if pend is not None:
                    pend()
                pend = y_evac(bank, co, TB)
            pend()
            postnorm_update(lat[b], 5)

        for b in range(NB):
            attention_block(b)
            merge_out(b)

    set_ws(())
    if stage >= 3:
        ffn_layer([[lat[0], lat[1]], [lat[2], lat[3]]], 1, 6, 8)

    for b in range(NB):
        dma("sp", out_d.ap().rearrange("(c p) t -> p c t", p=P)[:, :, b * TB:(b + 1) * TB], hT[:, :, b * TB:(b + 1) * TB],
            r=[B_h[c][b] for c in range(DC)], w=[B_out[b]], sem=S_out)
    pg.final.append(S_out)

    pg.finalize(esems)
    with nc.Block() as block:
        @block.tensor
        def _(e):
            pg.replay("pe", e)

        @block.scalar
        def _(e):
            pg.replay("act", e)

        @block.vector
        def _(e):
            pg.replay("dve", e)

        @block.gpsimd
        def _(e):
            pg.replay("pool", e)

        @block.sync
        def _(e):
            pg.replay("sp", e)
    es.close()
    return nc


def _panels(W, pw):
    K, N = W.shape
    kc, npn = K // P, N // pw
    return np.ascontiguousarray(W.reshape(kc, P, npn, pw).transpose(2, 1, 0, 3).reshape(npn, P, kc * pw))


def _chunked(vec):
    return np.ascontiguousarray(vec.reshape(-1, P).T)


def _rope_tables(t0):
    pos = np.arange(t0, t0 + T)
    row = (pos // 64).astype(np.float32)
    col = (pos % 64).astype(np.float32)
    inv_freq = (np.float32(10000.0) ** (np.float32(-2.0) * np.arange(16, dtype=np.float32) / np.float32(32))).astype(np.float32)
    ang = np.zeros((T, 64), np.float32)
    sgn = np.zeros((64,), np.float32)
    for d in range(64):
        axis, half, fr = d // 32, (d % 32) // 16, d % 16
        ang[:, d] = (row if axis == 0 else col) * inv_freq[fr]
        sgn[d] = -1.0 if half == 0 else 1.0
    cos = np.cos(ang).astype(np.float32)
    sin = (np.sin(ang).astype(np.float32)) * sgn[None, :]
    cT = np.ascontiguousarray(np.concatenate([cos.T, cos.T], 0))
    sT = np.ascontiguousarray(np.concatenate([sin.T, sin.T], 0))
    return cT, sT


def _perm_cols():
    idx = np.arange(1024)
    d = idx % 64
    half = (d % 32) // 16
    return np.where(half == 0, idx + 16, idx - 16)


_CACHE = {}


def kernel(x, c, ctx, c_ctx, w_mod, b_mod, ffn1_pre_g, ffn1_post_g, ffn1_w_gate, ffn1_w_up, ffn1_w_down,
           mix_pre_g, mix_post_g, w_in, lam_q1, lam_k1, lam_q2, lam_k2, attn_subln_g, conv_w,
           w_attn_proj, w_conv_proj, w_out, ffn2_pre_g, ffn2_post_g, ffn2_w_gate, ffn2_w_up, ffn2_w_down, _stage=3):
    f = lambda a: np.asarray(a, dtype=np.float32)
    x, c, ctx, c_ctx = f(x), f(c), f(ctx), f(c_ctx)
    w_in0 = f(w_in)[0]
    perm = _perm_cols()
    secs = {"q": w_in0[:, 0:1024], "k": w_in0[:, 1024:2048], "b": w_in0[:, 3072:4096], "c": w_in0[:, 4096:5120],
            "x": w_in0[:, 5120:6144], "ga": w_in0[:, 6144:7168], "gb": w_in0[:, 7168:8192]}
    secs["qp"] = secs["q"][:, perm]
    secs["kp"] = secs["k"][:, perm]
    wins = np.concatenate([_panels(secs[k], 128) for k in ("q", "qp", "k", "kp", "b", "c", "x", "ga", "gb")], 0)
    winv = _panels(w_in0[:, 2048:3072], 256)
    shared = {
        "wmod": _panels(f(w_mod)[0], 128),
        "bmod": np.ascontiguousarray(f(b_mod)[0].reshape(72, P).T),
        "gains": np.ascontiguousarray(np.stack([_chunked(f(g)[0]) for g in
                                                (ffn1_pre_g, ffn1_post_g, mix_pre_g, mix_post_g, ffn2_pre_g, ffn2_post_g)], 1)),
        "subg": np.ascontiguousarray(f(attn_subln_g)[0].reshape(P, 1)),
        "convw": np.ascontiguousarray(np.stack([_chunked(f(conv_w)[0][j]) for j in range(3)], 1)),
        "lamv": np.ascontiguousarray(np.broadcast_to(np.stack([f(lam_q1)[0], f(lam_k1)[0], f(lam_q2)[0], f(lam_k2)[0]], 0)[None], (P, 4, 64))),
        "w1g": _panels(f(ffn1_w_gate)[0], 128), "w1u": _panels(f(ffn1_w_up)[0], 128), "w1d": _panels(f(ffn1_w_down)[0], 128),
        "w2g": _panels(f(ffn2_w_gate)[0], 128), "w2u": _panels(f(ffn2_w_up)[0], 128), "w2d": _panels(f(ffn2_w_down)[0], 128),
        "wins": wins, "winv": winv,
        "wpa": _panels(f(w_attn_proj)[0], 128), "wpb": _panels(f(w_conv_proj)[0], 128), "wo": _panels(f(w_out)[0], 128),
    }
    in_maps = []
    for core in range(8):
        b, r = core // 4, core % 4
        t0 = r * T
        cT, sT = _rope_tables(t0)
        sel = np.zeros((P, 8), np.float32)
        if r > 0:
            sel[:, r - 1] = 1.0
        if r < 3:
            sel[:, 4 + r + 1] = 1.0
        m = dict(shared)
        m["xT"] = np.ascontiguousarray(x[b, t0:t0 + T, :].T)
        m["ctxT"] = np.ascontiguousarray(ctx[b, r * NCTX:(r + 1) * NCTX, :].T)
        m["cvec"] = np.ascontiguousarray(np.stack([_chunked(c[b]), _chunked(c_ctx)], -1))
        m["ropec"], m["ropes"], m["sel"] = cT, sT, sel
        in_maps.append(m)
    if _stage not in _CACHE:
        _CACHE[_stage] = build(_stage)
    nc = _CACHE[_stage]
    res = run_bass_kernel_spmd(nc, in_maps, core_ids=list(range(8)))
    out = np.empty((2, 4 * T, 1024), np.float32)
    for core in range(8):
        b, r = core // 4, core % 4
        out[b, r * T:(r + 1) * T, :] = np.asarray(res.results[core]["outT"]).T
    return out
```
